# Optimizing a Trainium2 kernel written in Bass

```python
import jax, jax.numpy as jnp
from jax import lax
import numpy as np

D_MODEL = 2048
BATCH = 4
SEQ = 4096
DEPTH = 2

MEM_LEN = 256
EPS = 1e-6
HEAD_DIM = 64
N_Q_HEADS = 16
N_KV_HEADS = 2
Q_PER_KV = N_Q_HEADS // N_KV_HEADS
ATTN_WIDTH = N_Q_HEADS * HEAD_DIM
KV_WIDTH = N_KV_HEADS * HEAD_DIM
WINDOW = 128
BLOCK = 128
ROPE_DIM = HEAD_DIM // 4
ROPE_THETA = 500000.0
SGU_GROUPS = 8
SGU_WIDTH = D_MODEL // 2
SGU_GROUP_DIM = SGU_WIDTH // SGU_GROUPS
CHUNK = 128
IN_WIDTH = ATTN_WIDTH + 2 * KV_WIDTH + 2 * SGU_WIDTH
MIX_WIDTH = ATTN_WIDTH + SGU_WIDTH
POOL_WINDOWS = (2, 4, 8, 16)
N_POOL_GROUPS = len(POOL_WINDOWS)
POOL_GROUP_DIM = D_MODEL // N_POOL_GROUPS
X_HEADS = 4
X_HEAD_DIM = 128
X_WIDTH = X_HEADS * X_HEAD_DIM
D_FF = 5632
N_NORMS = 8
N_EVEN = (DEPTH + 1) // 2
N_ODD = DEPTH // 2

kernel_name = "hybrid_swa_sgu_pool_macaron"


def rms_norm(x, g):
    xf = x.astype(jnp.float32)
    y = xf * lax.rsqrt(jnp.mean(xf * xf, axis=-1, keepdims=True) + EPS)
    return (y * g.astype(jnp.float32)).astype(x.dtype)


def swiglu(h, wg, wu, wd):
    return (jax.nn.silu(h @ wg) * (h @ wu)) @ wd


def rope_tables(seq):
    half = ROPE_DIM // 2
    inv = ROPE_THETA ** (-jnp.arange(half, dtype=jnp.float32) * 2.0 / ROPE_DIM)
    ang = jnp.arange(seq, dtype=jnp.float32)[:, None] * inv[None, :]
    return jnp.cos(ang)[:, None, :], jnp.sin(ang)[:, None, :]


def partial_rope(x, cos, sin):
    xf = x.astype(jnp.float32)
    half = ROPE_DIM // 2
    x1 = xf[..., :half]
    x2 = xf[..., half:ROPE_DIM]
    rot = jnp.concatenate([x1 * cos - x2 * sin, x2 * cos + x1 * sin, xf[..., ROPE_DIM:]], axis=-1)
    return rot.astype(x.dtype)


def swa_sink_attention(q, k, v, sinks):
    b, s = q.shape[0], q.shape[1]
    nb = s // BLOCK
    qb = q.reshape(b, nb, BLOCK, N_KV_HEADS, Q_PER_KV, HEAD_DIM)
    pad = ((0, 0), (BLOCK, 0), (0, 0), (0, 0))
    kp = jnp.pad(k, pad).reshape(b, nb + 1, BLOCK, N_KV_HEADS, HEAD_DIM)
    vp = jnp.pad(v, pad).reshape(b, nb + 1, BLOCK, N_KV_HEADS, HEAD_DIM)
    kb = jnp.concatenate([kp[:, :-1], kp[:, 1:]], axis=2)
    vb = jnp.concatenate([vp[:, :-1], vp[:, 1:]], axis=2)
    scores = jnp.einsum('bnqhgd,bnkhd->bnhgqk', qb, kb,
                        preferred_element_type=jnp.float32) * (HEAD_DIM ** -0.5)
    qi = jnp.arange(BLOCK)[:, None]
    kj = jnp.arange(2 * BLOCK)[None, :]
    rel = qi + BLOCK - kj
    band = (rel >= 0) & (rel < WINDOW)
    not_pad = (jnp.arange(nb)[:, None, None] > 0) | (kj >= BLOCK)[None]
    valid = band[None] & not_pad
    scores = jnp.where(valid[None, :, None, None], scores, jnp.float32(-1e30))
    sink = jnp.broadcast_to(
        sinks.astype(jnp.float32).reshape(N_KV_HEADS, Q_PER_KV)[None, None, :, :, None, None],
        scores.shape[:-1] + (1,))
    probs = jax.nn.softmax(jnp.concatenate([scores, sink], axis=-1), axis=-1)[..., :-1]
    out = jnp.einsum('bnhgqk,bnkhd->bnqhgd', probs.astype(v.dtype), vb)
    return out.reshape(b, s, ATTN_WIDTH)


def chunked_spatial_gating(u, v, ln_g, ln_b, w_s, b_s):
    b, s = u.shape[0], u.shape[1]
    nc = s // CHUNK
    vf = v.astype(jnp.float32)
    mu = jnp.mean(vf, axis=-1, keepdims=True)
    var = jnp.mean(jnp.square(vf - mu), axis=-1, keepdims=True)
    vn = ((vf - mu) * lax.rsqrt(var + EPS) * ln_g.astype(jnp.float32) + ln_b.astype(jnp.float32)).astype(v.dtype)
    vc = vn.reshape(b, nc, CHUNK, SGU_GROUPS, SGU_GROUP_DIM)
    causal = jnp.tril(jnp.ones((CHUNK, CHUNK), dtype=w_s.dtype))
    mixed = jnp.einsum('gij,bnjgc->bnigc', w_s * causal[None], vc) \
        + jnp.transpose(b_s)[None, None, :, :, None]
    return u * mixed.reshape(b, s, SGU_WIDTH).astype(u.dtype)


def attn_sgu_mixer(h, w_in, w_out, sinks, ln_g, ln_b, w_s, b_s, cos, sin):
    b, s = h.shape[0], h.shape[1]
    z = h @ w_in
    o1 = ATTN_WIDTH
    o2 = o1 + KV_WIDTH
    o3 = o2 + KV_WIDTH
    o4 = o3 + SGU_WIDTH
    q = partial_rope(z[..., :o1].reshape(b, s, N_Q_HEADS, HEAD_DIM), cos, sin)
    k = partial_rope(z[..., o1:o2].reshape(b, s, N_KV_HEADS, HEAD_DIM), cos, sin)
    v = z[..., o2:o3].reshape(b, s, N_KV_HEADS, HEAD_DIM)
    attn = swa_sink_attention(q, k, v, sinks)
    gate = chunked_spatial_gating(jax.nn.gelu(z[..., o3:o4]), jax.nn.gelu(z[..., o4:]), ln_g, ln_b, w_s, b_s)
    return jnp.concatenate([attn, gate], axis=-1) @ w_out


def multiscale_pool_mixer(h, pool_w, pool_scale):
    b, s = h.shape[0], h.shape[1]
    hf = h.astype(jnp.float32).reshape(b, s, N_POOL_GROUPS, POOL_GROUP_DIM)
    cs = jnp.cumsum(hf, axis=1)
    count = jnp.arange(1, s + 1, dtype=jnp.float32)
    outs = []
    for gi, w in enumerate(POOL_WINDOWS):
        c = cs[:, :, gi]
        prev = jnp.pad(c, ((0, 0), (w, 0), (0, 0)))[:, :s]
        mean = (c - prev) / jnp.minimum(count, jnp.float32(w))[None, :, None]
        outs.append(mean - hf[:, :, gi])
    pooled = jnp.stack(outs, axis=2).astype(h.dtype)
    y = jnp.einsum('bsgc,gcd->bsgd', pooled, pool_w).reshape(b, s, D_MODEL)
    return y * pool_scale


def memory_cross_attention(h, mem_n, wq, wk, wv, wo):
    b, s = h.shape[0], h.shape[1]
    m = mem_n.shape[1]
    q = (h @ wq).reshape(b, s, X_HEADS, X_HEAD_DIM)
    k = (mem_n @ wk).reshape(b, m, X_HEADS, X_HEAD_DIM)
    v = (mem_n @ wv).reshape(b, m, X_HEADS, X_HEAD_DIM)
    sc = jnp.einsum('bshd,bmhd->bhsm', q, k, preferred_element_type=jnp.float32) * (X_HEAD_DIM ** -0.5)
    p = jax.nn.softmax(sc, axis=-1)
    o = jnp.einsum('bhsm,bmhd->bshd', p.astype(v.dtype), v).reshape(b, s, X_WIDTH)
    return o @ wo


def setup_inputs(seed: int = 0) -> dict:
    key = jax.random.key(seed)
    ks = jax.random.split(key, 24)
    f32 = jnp.float32

    def w(k, shape, fan_in):
        return jax.random.normal(k, shape, f32) * (fan_in ** -0.5)

    return {
        "x": jax.random.normal(ks[0], (BATCH, SEQ, D_MODEL), f32),
        "mem": jax.random.normal(ks[1], (BATCH, MEM_LEN, D_MODEL), f32),
        "norms": 1.0 + 0.1 * jax.random.normal(ks[2], (DEPTH, N_NORMS, D_MODEL), f32),
        "mem_norm": 1.0 + 0.1 * jax.random.normal(ks[3], (DEPTH, D_MODEL), f32),
        "ffn1_wg": w(ks[4], (DEPTH, D_MODEL, D_FF), D_MODEL),
        "ffn1_wu": w(ks[5], (DEPTH, D_MODEL, D_FF), D_MODEL),
        "ffn1_wd": w(ks[6], (DEPTH, D_FF, D_MODEL), D_FF),
        "ffn2_wg": w(ks[7], (DEPTH, D_MODEL, D_FF), D_MODEL),
        "ffn2_wu": w(ks[8], (DEPTH, D_MODEL, D_FF), D_MODEL),
        "ffn2_wd": w(ks[9], (DEPTH, D_FF, D_MODEL), D_FF),
        "x_wq": w(ks[10], (DEPTH, D_MODEL, X_WIDTH), D_MODEL),
        "x_wk": w(ks[11], (DEPTH, D_MODEL, X_WIDTH), D_MODEL),
        "x_wv": w(ks[12], (DEPTH, D_MODEL, X_WIDTH), D_MODEL),
        "x_wo": w(ks[13], (DEPTH, X_WIDTH, D_MODEL), X_WIDTH),
        "mix_w_in": w(ks[14], (N_EVEN, D_MODEL, IN_WIDTH), D_MODEL),
        "mix_w_out": w(ks[15], (N_EVEN, MIX_WIDTH, D_MODEL), MIX_WIDTH),
        "attn_sinks": 0.5 * jax.random.normal(ks[16], (N_EVEN, N_Q_HEADS), f32),
        "sgu_ln_g": 1.0 + 0.1 * jax.random.normal(ks[17], (N_EVEN, SGU_WIDTH), f32),
        "sgu_ln_b": 0.02 * jax.random.normal(ks[18], (N_EVEN, SGU_WIDTH), f32),
        "sgu_w": w(ks[19], (N_EVEN, SGU_GROUPS, CHUNK, CHUNK), CHUNK),
        "sgu_b": 1.0 + 0.1 * jax.random.normal(ks[20], (N_EVEN, SGU_GROUPS, CHUNK), f32),
        "pool_w": w(ks[21], (N_ODD, N_POOL_GROUPS, POOL_GROUP_DIM, POOL_GROUP_DIM), POOL_GROUP_DIM),
        "pool_scale": 1.0 + 0.2 * jax.random.normal(ks[22], (N_ODD, D_MODEL), f32),
    }


def reference(x, mem, norms, mem_norm, ffn1_wg, ffn1_wu, ffn1_wd, ffn2_wg, ffn2_wu, ffn2_wd,
              x_wq, x_wk, x_wv, x_wo, mix_w_in, mix_w_out, attn_sinks, sgu_ln_g, sgu_ln_b,
              sgu_w, sgu_b, pool_w, pool_scale):
    cos, sin = rope_tables(x.shape[1])
    for layer in range(DEPTH):
        g = norms[layer]
        h = rms_norm(x, g[0])
        x = x + 0.5 * rms_norm(swiglu(h, ffn1_wg[layer], ffn1_wu[layer], ffn1_wd[layer]), g[1])
        h = rms_norm(x, g[2])
        i = layer // 2
        if layer % 2 == 0:
            m = attn_sgu_mixer(h, mix_w_in[i], mix_w_out[i], attn_sinks[i], sgu_ln_g[i], sgu_ln_b[i],
                               sgu_w[i], sgu_b[i], cos, sin)
        else:
            m = multiscale_pool_mixer(h, pool_w[i], pool_scale[i])
        x = x + rms_norm(m, g[3])
        h = rms_norm(x, g[4])
        mem_n = rms_norm(mem, mem_norm[layer])
        x = x + rms_norm(memory_cross_attention(h, mem_n, x_wq[layer], x_wk[layer], x_wv[layer], x_wo[layer]), g[5])
        h = rms_norm(x, g[6])
        x = x + 0.5 * rms_norm(swiglu(h, ffn2_wg[layer], ffn2_wu[layer], ffn2_wd[layer]), g[7])
    return x
```

```python
import numpy as np
import concourse.bass as bass
import concourse.mybir as mybir
from concourse.bass_utils import run_bass_kernel_spmd

F32 = mybir.dt.float32
BF16 = mybir.dt.bfloat16
AF = mybir.ActivationFunctionType
ALU = mybir.AluOpType
AX = mybir.AxisListType

D = 2048
KC = 16
DFF = 5632
NCORES = 8
SEQ = 4096
BATCH = 4
HALF = SEQ // 2
NBLK = 18
TOK = NBLK * 128
EPS = 1e-6
MEM = 256
SAME_RAW_SYNC = True


def _dtsize(dt):
    return 2 if dt == BF16 else 4


class Buf:
    __slots__ = ("w", "r", "dsem", "dcount", "name")

    def __init__(self, name=""):
        self.w = {}
        self.r = {}
        self.dsem = None
        self.dcount = 0
        self.name = name


class Eng:
    def __init__(self, K, eng, name, counter=True):
        self.K = K
        self.eng = eng
        self.name = name
        self.sem = K.newsem("c_" + name) if counter else None
        self.count = 0
        self.waited = {}

    def wait_all(self, evs):
        best = {}
        for ev in evs:
            key = id(ev[0])
            if key not in best or best[key][1] < ev[1]:
                best[key] = ev
        for key, ev in best.items():
            if self.waited.get(key, 0) >= ev[1]:
                continue
            self.eng.wait_ge(ev[0], ev[1])
            self.waited[key] = ev[1]


class K:
    def __init__(self, arena_words=53200):
        self.nc = bass.Bass("TRN2", target_bir_lowering=False)
        nc = self.nc
        self.sems = []
        self.PE = Eng(self, nc.tensor, "pe")
        self.ACT = Eng(self, nc.scalar, "act")
        self.DVE = Eng(self, nc.vector, "dve")
        self.POOL = Eng(self, nc.gpsimd, "pool")
        self.SP = Eng(self, nc.sync, "sp", counter=False)
        self.engs = [self.PE, self.ACT, self.DVE, self.POOL, self.SP]
        self.arena = nc.sbuf_tensor("arena", [128, arena_words], F32).__enter__()
        self.arena_words = arena_words
        self.top = 0
        self.bank_h = [nc.psum_tensor(f"bank{i}", [128, 512], F32).__enter__() for i in range(8)]
        self.banks = [h[:, :] for h in self.bank_h]
        self.bankbuf = [Buf(f"bank{i}") for i in range(8)]
        self.dma_bufs = []
        self.dram = {}

    def newsem(self, name):
        s = self.nc.semaphore(name).__enter__()
        self.sems.append(s)
        return s

    def alloc(self, shape, dt):
        n = 1
        for s in shape[1:]:
            n *= s
        words = (n * _dtsize(dt) + 3) // 4
        words = (words + 7) // 8 * 8
        off = self.top
        self.top += words
        assert self.top <= self.arena_words, f"arena overflow {self.top}"
        ap = self.arena[:, off:off + words]
        if dt != F32:
            ap = ap.bitcast(dt)
        ap = ap[:, 0:n]
        if len(shape) == 3:
            ap = ap.rearrange("p (a b) -> p a b", a=shape[1])
        elif len(shape) == 4:
            ap = ap.rearrange("p (a b c) -> p a b c", a=shape[1], b=shape[2])
        return ap

    def dmabuf(self, name):
        b = Buf(name)
        b.dsem = self.newsem("d_" + name)
        self.dma_bufs.append(b)
        return b

    def _deps(self, E, reads, writes, extra, nosame=False):
        deps = list(extra)
        for b in reads:
            for ev in b.w.values():
                if nosame and ev[2] is E:
                    continue
                deps.append(ev)
        for b in writes:
            for ev in list(b.w.values()) + list(b.r.values()):
                if nosame and ev[2] is E:
                    continue
                deps.append(ev)
        return deps

    @staticmethod
    def _mark(ev, reads, writes):
        key = id(ev[0])
        for b in reads:
            b.r[key] = ev
        for b in writes:
            b.w[key] = ev

    LIMIT = None
    ncalls = 0

    def _skip(self):
        K.ncalls += 1
        return K.LIMIT is not None and K.ncalls > K.LIMIT

    def op(self, E, fn, reads=(), writes=(), extra=(), nosame=False):
        if self._skip():
            return None
        E.wait_all(self._deps(E, reads, writes, extra, nosame))
        ins = fn()
        ins.then_inc(E.sem, 1)
        E.count += 1
        ev = (E.sem, E.count, E)
        self._mark(ev, reads, writes)
        return ev

    def mm(self, mms, reads=(), writes=(), extra=()):
        E = self.PE
        if self._skip():
            return None
        E.wait_all(self._deps(E, reads, writes, extra))
        ins = None
        for (o, l, r, st, sp) in mms:
            ins = self.nc.tensor.matmul(o, lhsT=l, rhs=r, start=st, stop=sp)
        ins.then_inc(E.sem, 1)
        E.count += 1
        ev = (E.sem, E.count, E)
        self._mark(ev, reads, writes)
        return ev

    def dma(self, Q, pairs, dbuf, reads=(), writes=(), extra=()):
        if self._skip():
            return None
        Q.wait_all(self._deps(Q, reads, writes, extra))
        for (o, i) in pairs:
            Q.eng.dma_start(out=o, in_=i).then_inc(dbuf.dsem, 16)
            dbuf.dcount += 16
        ev = (dbuf.dsem, dbuf.dcount, None)
        self._mark(ev, reads, writes)
        return ev

    def barrier(self, bufs=()):
        evs = []
        for E in self.engs:
            if E.sem is not None and E.count > 0:
                evs.append((E.sem, E.count, E))
        for b in self.dma_bufs:
            if b.dcount > 0:
                evs.append((b.dsem, b.dcount, None))
        for E in self.engs:
            E.wait_all([ev for ev in evs if ev[2] is not E])

    def finish(self):
        self.barrier()


def chunks_of(lst, n):
    return [lst[i:i + n] for i in range(0, len(lst), n)]


def pipeline(items, stages):
    n, d = len(items), len(stages)
    for t in range(n + d - 1):
        for s_ in range(d):
            i = t - s_
            if 0 <= i < n:
                stages[s_](items[i])


class Prog:
    def __init__(self, nblk=NBLK, dff=DFF, debug_out=None):
        self.k = K()
        k = self.k
        nc = k.nc
        self.nblk = nblk
        self.dff = dff
        self.fc = dff // 128
        self.ones = k.alloc([128, 128], BF16)
        self.gains = k.alloc([128, 2 * 8 * 16], F32)
        self.eps_t = k.alloc([128, 1], F32)
        self.wslot = [k.alloc([128, 8192], BF16) for _ in range(2)]
        self.wbuf = [k.dmabuf(f"w{i}") for i in range(2)]
        self.wstep = 0
        self.cbuf = k.dmabuf("consts")
        self.xblk = [k.alloc([128, KC, 128], F32) for _ in range(2)]
        self.b_x = [k.dmabuf("xblk0"), k.dmabuf("xblk1")]
        self.xcnt = 0
        self.gb = [k.dmabuf(f"g{i}") for i in range(4)]
        self.persist_top = k.top

    def din(self, name, shape, dt=F32):
        t = self.k.nc.dram_tensor(name, list(shape), dt, kind="ExternalInput").ap()
        self.k.dram[name] = t
        return t

    def dout(self, name, shape, dt=F32):
        t = self.k.nc.dram_tensor(name, list(shape), dt, kind="ExternalOutput").ap()
        self.k.dram[name] = t
        return t

    def dscratch(self, name, shape, dt=F32):
        t = self.k.nc.dram_tensor(name, list(shape), dt, kind="Internal").ap()
        self.k.dram[name] = t
        return t

    def load_consts(self):
        k = self.k
        nc = k.nc
        k.op(k.DVE, lambda: nc.vector.memset(self.ones, 1.0), writes=[self.cbuf])
        k.op(k.DVE, lambda: nc.vector.memset(self.eps_t, EPS), writes=[self.cbuf])
        k.dma(k.SP, [(self.gains, k.dram["norms_t"])], self.cbuf, writes=[self.cbuf])

    def gain(self, l, n, kc):
        c = (l * 8 + n) * 16 + kc
        return self.gains[:, c:c + 1]

    def wload(self, pairs_fn):
        k = self.k
        s = self.wstep % 2
        self.wstep += 1
        k.dma(k.POOL, pairs_fn(self.wslot[s]), self.wbuf[s], writes=[self.wbuf[s]])
        return s

    def ffn(self, l, which, Xin, Xout, blocks, xout_index=None):
        k = self.k
        nc = k.nc
        fc = self.fc
        wg = k.dram[f"ffn{which}_wg"][l].rearrange("(c p) f -> p c f", p=128)
        wu = k.dram[f"ffn{which}_wu"][l].rearrange("(c p) f -> p c f", p=128)
        wd = k.dram[f"ffn{which}_wd"][l].rearrange("(j p) d -> p j d", p=128)
        npre, npost = (0, 1) if which == 1 else (6, 7)
        mark = k.top
        STB = 6
        STOK = STB * 128
        hT = k.alloc([128, KC, STOK], BF16)
        A = k.alloc([128, fc, STOK], BF16)
        Y = k.alloc([128, KC, STOK], F32)
        xblk = self.xblk
        sq = k.alloc([128, KC, 128], BF16)
        srt = k.alloc([128, 128], F32)
        Rpre = k.alloc([128, 128], F32)
        RY = k.alloc([128, STOK], F32)
        srtY = k.alloc([128, 384], F32)
        sg = [k.alloc([128, 384], BF16) for _ in range(2)]
        sqY = [k.alloc([128, 384], BF16) for _ in range(2)]
        b_hT = [Buf("hT0"), Buf("hT1")]
        b_A = [Buf("A0"), Buf("A1")]
        b_Y = [Buf("Y0"), Buf("Y1")]
        b_x = self.b_x
        b_sq, b_srt, b_R, b_RY, b_srtY, b_tmp = Buf(), Buf(), Buf(), Buf(), Buf(), Buf()
        b_sg = [Buf(), Buf()]
        b_sqY = [Buf(), Buf()]
        b_ss = [Buf() for _ in range(4)]
        bk = k.banks
        bb = k.bankbuf
        st = {"x": 0, "sg": 0, "sqY": 0, "pb": 0, "nb": 0}

        supers = chunks_of(blocks, STB)

        def subtiles(sblocks):
            res = []
            for i in range(0, len(sblocks), 3):
                w = min(3, len(sblocks) - i) * 128
                res.append((i * 128, w))
            return res

        def prenorm_block(sblocks, bi):
            b = sblocks[bi]
            xs = st["x"] % 2
            st["x"] += 1
            k.dma(k.SP, [(xblk[xs], Xin[b])], b_x[xs], writes=[b_x[xs]])
            k.op(k.ACT, lambda: nc.scalar.activation(out=sq, in_=xblk[xs], func=AF.Square),
                 reads=[b_x[xs]], writes=[b_sq])
            nb = st["nb"] % 4
            st["nb"] += 1
            ssr = bk[6][:, nb * 128:(nb + 1) * 128]
            k.mm([(ssr, self.ones, sq[:, c, :], c == 0, c == KC - 1) for c in range(KC)],
                 reads=[b_sq, self.cbuf], writes=[b_ss[nb]])
            k.op(k.ACT, lambda: nc.scalar.activation(out=srt, in_=ssr, func=AF.Sqrt,
                                                     bias=self.eps_t, scale=1.0 / D),
                 reads=[b_ss[nb], self.cbuf], writes=[b_srt])
            k.op(k.DVE, lambda: nc.vector.reciprocal(out=Rpre, in_=srt), reads=[b_srt], writes=[b_R])
            n = bi // 3
            for c in range(KC):
                k.op(k.DVE, lambda c=c: nc.vector.scalar_tensor_tensor(
                    out=hT[:, c, bi * 128:(bi + 1) * 128], in0=xblk[xs][:, c, :],
                    scalar=self.gain(l, npre, c), in1=Rpre, op0=ALU.mult, op1=ALU.mult),
                    reads=[b_x[xs], b_R, self.cbuf], writes=[b_hT[n]], nosame=(c > 0))

        tail_state = {}

        def tail_a(sblocks, bi):
            b = sblocks[bi]
            n = bi // 3
            xs = st["x"] % 2
            st["x"] += 1
            tail_state[(id(sblocks), bi)] = xs
            k.dma(k.SP, [(xblk[xs], Xin[b])], b_x[xs], writes=[b_x[xs]])
            for c in range(0, KC // 2):
                k.op(k.DVE, lambda c=c: nc.vector.scalar_tensor_tensor(
                    out=Y[:, c, bi * 128:(bi + 1) * 128], in0=Y[:, c, bi * 128:(bi + 1) * 128],
                    scalar=self.gain(l, npost, c), in1=RY[:, bi * 128:(bi + 1) * 128],
                    op0=ALU.mult, op1=ALU.mult),
                    reads=[b_Y[n], b_RY, self.cbuf], writes=[b_Y[n]], nosame=(c > 0))

        def tail_b(sblocks, bi):
            b = sblocks[bi]
            n = bi // 3
            xs = tail_state.pop((id(sblocks), bi))
            for c in range(KC // 2, KC):
                k.op(k.DVE, lambda c=c: nc.vector.scalar_tensor_tensor(
                    out=Y[:, c, bi * 128:(bi + 1) * 128], in0=Y[:, c, bi * 128:(bi + 1) * 128],
                    scalar=self.gain(l, npost, c), in1=RY[:, bi * 128:(bi + 1) * 128],
                    op0=ALU.mult, op1=ALU.mult),
                    reads=[b_Y[n], b_RY, self.cbuf], writes=[b_Y[n]], nosame=True)
            k.op(k.DVE, lambda: nc.vector.scalar_tensor_tensor(
                out=xblk[xs], in0=Y[:, :, bi * 128:(bi + 1) * 128], scalar=0.5, in1=xblk[xs],
                op0=ALU.mult, op1=ALU.add),
                reads=[b_Y[n], b_x[xs]], writes=[b_x[xs]])
            k.dma(k.SP, [(Xout[b if xout_index is None else xout_index(b)], xblk[xs])], b_x[xs], reads=[b_x[xs]])

        def tail_block(sblocks, bi):
            tail_a(sblocks, bi)
            tail_b(sblocks, bi)

        steps = []
        for si, sb in enumerate(supers):
            for jj in range(fc // 2):
                steps.append(("gu", si, jj))
            for i in range(KC):
                steps.append(("d", si, i))

        def issue(step):
            kind, si, idx = step
            if kind == "gu":
                def pf(slot):
                    v = slot.rearrange("p (m c f) -> p m c f", m=2, c=KC)
                    return [(v[:, 0], wg[:, :, idx * 256:(idx + 1) * 256]),
                            (v[:, 1], wu[:, :, idx * 256:(idx + 1) * 256])]
                return self.wload(pf)
            else:
                def pf(slot):
                    v = slot[:, 0:fc * 128].rearrange("p (j d) -> p j d", j=fc)
                    return [(v, wd[:, :, idx * 128:(idx + 1) * 128])]
                return self.wload(pf)

        for bi in range(len(supers[0])):
            prenorm_block(supers[0], bi)
        slot_of = {0: issue(steps[0])}
        pend_ss = None
        for t, step in enumerate(steps):
            if t + 1 < len(steps):
                slot_of[t + 1] = issue(steps[t + 1])
            kind, si, idx = step
            sb = supers[si]
            subs = subtiles(sb)
            ws = slot_of[t]
            wv = self.wslot[ws]
            if kind == "gu":
                v = wv.rearrange("p (m c f) -> p m c f", m=2, c=KC)
                for jl in range(2):
                    j = idx * 2 + jl
                    for n, (off, w) in enumerate(subs):
                        pb = (st["pb"] % 2) * 2
                        st["pb"] += 1
                        G = bk[pb][:, 0:w]
                        U = bk[pb + 1][:, 0:w]
                        k.mm([(G, v[:, 0, c, jl * 128:(jl + 1) * 128], hT[:, c, off:off + w], c == 0, c == KC - 1)
                              for c in range(KC)], reads=[self.wbuf[ws], b_hT[n]], writes=[bb[pb]])
                        k.mm([(U, v[:, 1, c, jl * 128:(jl + 1) * 128], hT[:, c, off:off + w], c == 0, c == KC - 1)
                              for c in range(KC)], reads=[self.wbuf[ws], b_hT[n]], writes=[bb[pb + 1]])
                        ss = st["sg"] % 2
                        st["sg"] += 1
                        k.op(k.ACT, lambda: nc.scalar.activation(out=sg[ss][:, 0:w], in_=G, func=AF.Silu),
                             reads=[bb[pb]], writes=[b_sg[ss]])
                        k.op(k.DVE, lambda: nc.vector.tensor_tensor(out=A[:, j, off:off + w], in0=sg[ss][:, 0:w],
                                                                   in1=U, op=ALU.mult),
                             reads=[b_sg[ss], bb[pb + 1]], writes=[b_A[n]])
                    if si > 0 and idx < len(supers[si - 1]):
                        (tail_a if jl == 0 else tail_b)(supers[si - 1], idx)
                if si > 0 and idx == fc // 2 - 1:
                    for bi in range(fc // 2, len(supers[si - 1])):
                        tail_block(supers[si - 1], bi)
            else:
                i = idx
                v = wv[:, 0:fc * 128].rearrange("p (j d) -> p j d", j=fc)
                for n, (off, w) in enumerate(subs):
                    pb = st["pb"] % 4
                    st["pb"] += 1
                    Yp = bk[pb][:, 0:w]
                    k.mm([(Yp, v[:, j, :], A[:, j, off:off + w], j == 0, j == fc - 1) for j in range(fc)],
                         reads=[self.wbuf[ws], b_A[n]], writes=[bb[pb]])
                    if pend_ss is not None:
                        pi, pn, pw, pq = pend_ss
                        k.mm([(bk[4 + pn][:, 0:pw], self.ones, sqY[pq][:, 0:pw], pi == 0, pi == KC - 1)],
                             reads=[b_sqY[pq], self.cbuf], writes=[bb[4 + pn]])
                        pend_ss = None
                    k.op(k.DVE, lambda: nc.vector.tensor_copy(out=Y[:, i, off:off + w], in_=Yp),
                         reads=[bb[pb]], writes=[b_Y[n]])
                    q = st["sqY"] % 2
                    st["sqY"] += 1
                    k.op(k.ACT, lambda: nc.scalar.activation(out=sqY[q][:, 0:w], in_=Y[:, i, off:off + w], func=AF.Square),
                         reads=[b_Y[n]], writes=[b_sqY[q]])
                    pend_ss = (i, n, w, q)
                if si + 1 < len(supers) and i < len(supers[si + 1]):
                    prenorm_block(supers[si + 1], i)
                if i == KC - 1:
                    pi, pn, pw, pq = pend_ss
                    k.mm([(bk[4 + pn][:, 0:pw], self.ones, sqY[pq][:, 0:pw], pi == 0, pi == KC - 1)],
                         reads=[b_sqY[pq], self.cbuf], writes=[bb[4 + pn]])
                    pend_ss = None
                    for n, (off, w) in enumerate(subs):
                        k.op(k.ACT, lambda: nc.scalar.activation(out=srtY[:, 0:w], in_=bk[4 + n][:, 0:w], func=AF.Sqrt,
                                                                 bias=self.eps_t, scale=1.0 / D),
                             reads=[bb[4 + n], self.cbuf], writes=[b_srtY])
                        k.op(k.DVE, lambda: nc.vector.reciprocal(out=RY[:, off:off + w], in_=srtY[:, 0:w]),
                             reads=[b_srtY], writes=[b_RY])
                    if si == len(supers) - 1:
                        for bi in range(len(sb)):
                            tail_block(sb, bi)
        k.barrier()
        k.top = mark

    def norm_scratch(self):
        k = self.k
        ns = {}
        ns["sq"] = k.alloc([128, KC, 128], BF16)
        ns["srt"] = k.alloc([128, 128], F32)
        ns["R"] = k.alloc([128, 128], F32)
        ns["b_sq"], ns["b_srt"], ns["b_R"] = Buf(), Buf(), Buf()
        ns["b_ss"] = [Buf() for _ in range(4)]
        ns["nb"] = 0
        return ns

    def prenorm_to(self, ns, Xin, b, l, nidx, dst_of_chunk, dst_buf, flag=None, flag_buf=None):
        k = self.k
        nc = k.nc
        xs = self.xcnt % 2
        self.xcnt += 1
        xblk, b_x = self.xblk, self.b_x
        k.dma(k.SP, [(xblk[xs], Xin[b])], b_x[xs], writes=[b_x[xs]])
        k.op(k.ACT, lambda: nc.scalar.activation(out=ns["sq"], in_=xblk[xs], func=AF.Square),
             reads=[b_x[xs]], writes=[ns["b_sq"]])
        nb = ns["nb"] % 4
        ns["nb"] += 1
        ssr = k.banks[6][:, nb * 128:(nb + 1) * 128]
        k.mm([(ssr, self.ones, ns["sq"][:, c, :], c == 0, c == KC - 1) for c in range(KC)],
             reads=[ns["b_sq"], self.cbuf], writes=[ns["b_ss"][nb], k.bankbuf[6]])
        k.op(k.ACT, lambda: nc.scalar.activation(out=ns["srt"], in_=ssr, func=AF.Sqrt,
                                                 bias=self.eps_t, scale=1.0 / D),
             reads=[ns["b_ss"][nb], k.bankbuf[6], self.cbuf], writes=[ns["b_srt"]])
        k.op(k.DVE, lambda: nc.vector.reciprocal(out=ns["R"], in_=ns["srt"]),
             reads=[ns["b_srt"]], writes=[ns["b_R"]])
        if flag is not None:
            k.op(k.DVE, lambda: nc.vector.tensor_scalar(out=ns["R"], in0=ns["R"], scalar1=flag, scalar2=None,
                                                       op0=ALU.mult),
                 reads=[ns["b_R"], self.cbuf, flag_buf], writes=[ns["b_R"]])
        for c in range(KC):
            k.op(k.DVE, lambda c=c: nc.vector.scalar_tensor_tensor(
                out=dst_of_chunk(c), in0=xblk[xs][:, c, :],
                scalar=self.gain(l, nidx, c), in1=ns["R"], op0=ALU.mult, op1=ALU.mult),
                reads=[b_x[xs], ns["b_R"], self.cbuf], writes=[dst_buf], nosame=(c > 0))

    def tail_scratch(self, Y=None, ntok=768):
        k = self.k
        ts = {}
        ts["Y"] = Y if Y is not None else k.alloc([128, KC, 768], F32)
        ts["RY"] = k.alloc([128, ntok], F32)
        ts["srtY"] = k.alloc([128, 384], F32)
        ts["sqY"] = [k.alloc([128, 384], BF16) for _ in range(2)]
        ts["b_Y"] = [Buf(), Buf()]
        ts["b_RY"], ts["b_srtY"] = Buf(), Buf()
        ts["b_sqY"] = [Buf(), Buf()]
        ts["q"] = 0
        ts["pb"] = 0
        return ts

    @staticmethod
    def subtiles(nblocks):
        res = []
        for i in range(0, nblocks, 3):
            res.append((i * 128, min(3, nblocks - i) * 128))
        return res

    def proj_tail(self, sblocks, inT, b_in, groups, wpairs, mmlist, l, npost, factor, Xin, Xout, ts,
                  scale_of=None, xout_index=None, xreads=(), interleave=None):
        k = self.k
        nc = k.nc
        bk, bb = k.banks, k.bankbuf
        Y, RY, srtY, sqY = ts["Y"], ts["RY"], ts["srtY"], ts["sqY"]
        subs = self.subtiles(len(sblocks))
        xblk, b_x = self.xblk, self.b_x
        slot = self.wload(lambda sl: wpairs(sl, groups[0]))
        pend = None

        def flush():
            pi, pn, pw, pq = pend
            k.mm([(bk[4 + pn][:, 0:pw], self.ones, sqY[pq][:, 0:pw], pi == 0, pi == KC - 1)],
                 reads=[ts["b_sqY"][pq], self.cbuf], writes=[bb[4 + pn]])

        for gi, ocs in enumerate(groups):
            cur = slot
            if gi + 1 < len(groups):
                slot = self.wload(lambda sl: wpairs(sl, groups[gi + 1]))
            for oi, oc in enumerate(ocs):
                lst = mmlist(self.wslot[cur], oc, oi)
                for n, (off, w) in enumerate(subs):
                    pb = ts["pb"] % 4
                    ts["pb"] += 1
                    Yp = bk[pb][:, 0:w]
                    k.mm([(Yp, lh, inT[:, ic, off:off + w], j == 0, j == len(lst) - 1)
                          for j, (lh, ic) in enumerate(lst)],
                         reads=[self.wbuf[cur], b_in[n]], writes=[bb[pb]])
                    if pend is not None:
                        flush()
                        pend = None
                    if scale_of is None:
                        k.op(k.DVE, lambda: nc.vector.tensor_copy(out=Y[:, oc, off:off + w], in_=Yp),
                             reads=[bb[pb]], writes=[ts["b_Y"][n]])
                    else:
                        k.op(k.DVE, lambda: nc.vector.tensor_scalar(out=Y[:, oc, off:off + w], in0=Yp,
                                                                   scalar1=scale_of(oc), scalar2=None, op0=ALU.mult),
                             reads=[bb[pb], self.cbuf] + list(xreads), writes=[ts["b_Y"][n]])
                    q = ts["q"] % 2
                    ts["q"] += 1
                    k.op(k.ACT, lambda: nc.scalar.activation(out=sqY[q][:, 0:w], in_=Y[:, oc, off:off + w],
                                                             func=AF.Square),
                         reads=[ts["b_Y"][n]], writes=[ts["b_sqY"][q]])
                    pend = (oc, n, w, q)
                if interleave:
                    interleave.pop(0)()
        while interleave:
            interleave.pop(0)()
        flush()
        pend = None
        for n, (off, w) in enumerate(subs):
            k.op(k.ACT, lambda: nc.scalar.activation(out=srtY[:, 0:w], in_=bk[4 + n][:, 0:w], func=AF.Sqrt,
                                                     bias=self.eps_t, scale=1.0 / D),
                 reads=[bb[4 + n], self.cbuf], writes=[ts["b_srtY"]])
            k.op(k.DVE, lambda: nc.vector.reciprocal(out=RY[:, off:off + w], in_=srtY[:, 0:w]),
                 reads=[ts["b_srtY"]], writes=[ts["b_RY"]])
        for bi, b in enumerate(sblocks):
            n = bi // 3
            xs = self.xcnt % 2
            self.xcnt += 1
            k.dma(k.SP, [(xblk[xs], Xin[b])], b_x[xs], writes=[b_x[xs]])
            for c in range(KC):
                k.op(k.DVE, lambda c=c: nc.vector.scalar_tensor_tensor(
                    out=Y[:, c, bi * 128:(bi + 1) * 128], in0=Y[:, c, bi * 128:(bi + 1) * 128],
                    scalar=self.gain(l, npost, c), in1=RY[:, bi * 128:(bi + 1) * 128],
                    op0=ALU.mult, op1=ALU.mult),
                    reads=[ts["b_Y"][n], ts["b_RY"], self.cbuf], writes=[ts["b_Y"][n]], nosame=(c > 0))
            k.op(k.DVE, lambda: nc.vector.scalar_tensor_tensor(
                out=xblk[xs], in0=Y[:, :, bi * 128:(bi + 1) * 128], scalar=float(factor), in1=xblk[xs],
                op0=ALU.mult, op1=ALU.add),
                reads=[ts["b_Y"][n], b_x[xs]], writes=[b_x[xs]])
            ob = b if xout_index is None else xout_index(b)
            k.dma(k.SP, [(Xout[ob], xblk[xs])], b_x[xs], reads=[b_x[xs]])

    def mixer0(self, Xin, Xout, out_blocks):
        k = self.k
        nc = k.nc
        bk, bb = k.banks, k.bankbuf
        l = 0
        mark = k.top
        dr = k.dram
        win = dr["mix_w_in"][0].rearrange("(c p) f -> p c f", p=128)
        wout = dr["mix_w_out"][0].rearrange("(c p) f -> p c f", p=128)
        nblk_all = self.nblk
        ropeP = k.alloc([128, 128], BF16)
        ident = k.alloc([128, 128], BF16)
        maskC = k.alloc([128, 4, 128], BF16)
        maskP = k.alloc([128, 4, 128], BF16)
        maskP2 = k.alloc([128, 4, 128], BF16)
        WmT = k.alloc([128, 8, 128], BF16)
        bsb = k.alloc([128, 8, 128], F32)
        lng = k.alloc([128, 1024], F32)
        lnb = k.alloc([128, 1024], F32)
        esink = k.alloc([128, 16], F32)
        kr = k.alloc([128, 2, nblk_all * 128], BF16)
        vdup = k.alloc([128, nblk_all, 256], BF16)
        b_kr, b_vdup = Buf(), Buf()
        STM = 4
        MT = STM * 128
        R0 = k.alloc([128, KC, MT], F32)
        r0b = R0.rearrange("p a b -> p (a b)")
        hT = k.alloc([128, KC, MT], BF16)
        uT = r0b[:, 8 * MT:16 * MT].rearrange("p (a b) -> p a b", a=8)
        wsf = r0b[:, 0:1024].rearrange("p (a b) -> p a b", a=8)
        trilf = r0b[:, 1024:1152]
        ropC = k.alloc([128, MT], F32)
        ropS = k.alloc([128, MT], F32)
        qr = k.alloc([128, 8, MT], BF16)
        vg = [k.alloc([128, 1024], F32) for _ in range(2)]
        vn = k.alloc([128, STM, 1024], BF16)
        mixT = k.alloc([128, KC, MT], BF16)
        zb = [k.alloc([128, 384], BF16) for _ in range(2)]
        t1 = [k.alloc([128, 512], F32) for _ in range(2)]
        t2 = [k.alloc([128, 384], F32) for _ in range(2)]
        ET = [k.alloc([128, 2, 512], BF16) for _ in range(2)]
        dn = [k.alloc([128, 512], F32) for _ in range(2)]
        rden = [k.alloc([128, 512], F32) for _ in range(2)]
        stats = k.alloc([128, 12], F32)
        mv = k.alloc([128, 2], F32)
        rstd = k.alloc([128, 1], F32)
        gt = t1
        ns = self.norm_scratch()
        ts = self.tail_scratch(Y=R0, ntok=MT)
        b_hT, b_uT, b_rope, b_qr, b_vn, b_mix = Buf(), Buf(), self.gb[2], Buf(), Buf(), Buf()
        ts["b_Y"] = [b_uT, b_uT]
        b_vg = [Buf(), Buf()]
        b_zb, b_t1, b_t2 = [Buf(), Buf()], [Buf(), Buf()], [Buf(), Buf()]
        b_ET, b_dn, b_rden = [Buf(), Buf()], [Buf(), Buf()], [Buf(), Buf()]
        b_st, b_gt = Buf(), b_t1
        cnt = {"pb": 0, "r": 0, "a": 0, "g": 0, "v": 0}
        cb = self.gb[0]
        pr = [(ropeP, dr["ropeP"]), (ident, dr["ident"])]
        for i in range(4):
            pr += [(maskC[:, i, :], dr["maskC"]), (maskP[:, i, :], dr["maskP"]), (maskP2[:, i, :], dr["maskP2"])]
        k.dma(k.POOL, pr, self.gb[3], writes=[cb])
        k.dma(k.SP, [(wsf, dr["sgu_wT"]), (trilf, dr["tril"]), (bsb, dr["bs_b"]), (lng, dr["lng_b"]),
                     (lnb, dr["lnb_b"]), (esink, dr["sinks_b"])], self.gb[1], writes=[cb])
        for g in range(8):
            k.op(k.DVE, lambda g=g: nc.vector.tensor_tensor(out=WmT[:, g, :], in0=wsf[:, g, :], in1=trilf, op=ALU.mult),
                 reads=[cb], writes=[cb])
        k.op(k.ACT, lambda: nc.scalar.activation(out=esink, in_=esink, func=AF.Exp), reads=[cb], writes=[cb])
        k.op(k.DVE, lambda: nc.vector.memset(vdup, 0.0), writes=[b_vdup])
        k.op(k.DVE, lambda: nc.vector.memset(kr, 0.0), writes=[b_kr])


        blocks = list(range(out_blocks[0] - 1, out_blocks[-1] + 1))
        supers = chunks_of(blocks, STM)

        def pre(sb_):
            return [lambda bi=bi, b=b: self.prenorm_to(ns, Xin, b, l, 2,
                                                       lambda c, bi=bi: hT[:, c, bi * 128:(bi + 1) * 128], b_hT)
                    for bi, b in enumerate(sb_)]
        for f_ in pre(supers[0]):
            f_()
        for si, sb in enumerate(supers):
            nsb = len(sb)
            tok0 = sb[0] * 128
            ntok = nsb * 128
            subs = self.subtiles(nsb)
            k.op(k.DVE, lambda: nc.vector.memset(mixT, 0.0), writes=[b_mix])
            k.dma(k.SP, [(ropC[:, 0:ntok], dr["ropeC"][:, tok0:tok0 + ntok]),
                         (ropS[:, 0:ntok], dr["ropeS"][:, tok0:tok0 + ntok])], b_rope, writes=[b_rope])
            clist = [("q", c) for c in range(8)] + [("k", h) for h in range(2)] + [("u", c) for c in range(8)]
            cgroups = chunks_of(clist, 4)

            def pairs_fm(sl, grp):
                v = sl.rearrange("p (kc ci col) -> p kc ci col", kc=KC, ci=4)
                pr = []
                for ci, (kind, idx) in enumerate(grp):
                    if kind == "q":
                        pr.append((v[:, :, ci, :], win[:, :, idx * 128:(idx + 1) * 128]))
                    elif kind == "u":
                        pr.append((v[:, :, ci, :], win[:, :, 1280 + idx * 128:1280 + (idx + 1) * 128]))
                    else:
                        for r in range(2):
                            pr.append((v[:, :, ci, r * 64:(r + 1) * 64], win[:, :, 1024 + idx * 64:1024 + (idx + 1) * 64]))
                return pr

            items = []
            for gi, grp in enumerate(cgroups):
                for ci, (kind, idx) in enumerate(grp):
                    for n, (off, w) in enumerate(subs):
                        items.append(dict(gi=gi, ci=ci, kind=kind, idx=idx, off=off, w=w, first=(ci == 0 and n == 0)))
            slots = {0: self.wload(lambda sl: pairs_fm(sl, cgroups[0]))}

            def fm_s0(it):
                gi = it["gi"]
                if it["first"] and gi + 1 < len(cgroups):
                    slots[gi + 1] = self.wload(lambda sl: pairs_fm(sl, cgroups[gi + 1]))
                cur = slots[gi]
                v = self.wslot[cur].rearrange("p (kc ci col) -> p kc ci col", kc=KC, ci=4)
                pb = cnt["pb"] % 4
                cnt["pb"] += 1
                it["pb"] = pb
                off, w = it["off"], it["w"]
                Z = bk[pb][:, 0:w]
                k.mm([(Z, v[:, c, it["ci"], :], hT[:, c, off:off + w], c == 0, c == KC - 1) for c in range(KC)],
                     reads=[self.wbuf[cur], b_hT], writes=[bb[pb]])
                if it["kind"] == "u":
                    k.op(k.ACT, lambda: nc.scalar.activation(out=uT[:, it["idx"], off:off + w], in_=Z,
                                                             func=AF.Gelu_apprx_tanh),
                         reads=[bb[pb]], writes=[b_uT])

            def fm_s1(it):
                if it["kind"] == "u":
                    return
                r = cnt["r"] % 2
                cnt["r"] += 1
                it["r"] = r
                pb, off, w = it["pb"], it["off"], it["w"]
                Z = bk[pb][:, 0:w]
                k.op(k.DVE, lambda: nc.vector.tensor_copy(out=zb[r][:, 0:w], in_=Z),
                     reads=[bb[pb]], writes=[b_zb[r]])
                k.op(k.DVE, lambda: nc.vector.tensor_tensor(out=t1[r][:, 0:w], in0=Z, in1=ropC[:, off:off + w],
                                                           op=ALU.mult),
                     reads=[bb[pb], b_rope], writes=[b_t1[r]])
                k.mm([(bk[4 + r][:, 0:w], ropeP, zb[r][:, 0:w], True, True)], reads=[b_zb[r], cb], writes=[bb[4 + r]])

            def fm_s2(it):
                if it["kind"] == "u":
                    return
                r, off, w, idx = it["r"], it["off"], it["w"], it["idx"]
                k.op(k.DVE, lambda: nc.vector.tensor_tensor(out=t2[r][:, 0:w], in0=bk[4 + r][:, 0:w],
                                                           in1=ropS[:, off:off + w], op=ALU.mult),
                     reads=[bb[4 + r], b_rope], writes=[b_t2[r]])
                if it["kind"] == "q":
                    dst, dbuf = qr[:, idx, off:off + w], b_qr
                else:
                    dst, dbuf = kr[:, idx, tok0 + off:tok0 + off + w], b_kr
                k.op(k.DVE, lambda: nc.vector.tensor_tensor(out=dst, in0=t1[r][:, 0:w], in1=t2[r][:, 0:w], op=ALU.add),
                     reads=[b_t1[r], b_t2[r]], writes=[dbuf])
            pipeline(items, [fm_s0, fm_s1, fm_s2])
            def pairs_v(sl):
                v = sl[:, 0:KC * 256].rearrange("p (kc col) -> p kc col", kc=KC)
                pr = []
                for h in range(2):
                    for r in range(2):
                        pr.append((v[:, :, h * 128 + r * 64:h * 128 + (r + 1) * 64],
                                   win[:, :, 1152 + h * 64:1152 + (h + 1) * 64]))
                return pr
            cur = self.wload(pairs_v)
            v = self.wslot[cur][:, 0:KC * 256].rearrange("p (kc col) -> p kc col", kc=KC)
            for bi, b in enumerate(sb):
                pb = cnt["pb"] % 4
                cnt["pb"] += 1
                Z = bk[pb][:, 0:256]
                k.mm([(Z, hT[:, c, bi * 128:(bi + 1) * 128], v[:, c, :], c == 0, c == KC - 1) for c in range(KC)],
                     reads=[self.wbuf[cur], b_hT], writes=[bb[pb]])
                k.op(k.ACT, lambda: nc.scalar.activation(out=vdup[:, b, :], in_=Z, func=AF.Copy),
                     reads=[bb[pb]], writes=[b_vdup])
            def pairs_sv(sl, hh):
                v = sl.rearrange("p (kc col) -> p kc col", kc=KC)
                return [(v, win[:, :, 2304 + hh * 512:2304 + (hh + 1) * 512])]
            sv = [self.wload(lambda sl, hh=hh: pairs_sv(sl, hh)) for hh in range(2)]

            def sg_s0(it):
                bi = it["bi"]
                vi = cnt["v"] % 2
                cnt["v"] += 1
                it["vi"] = vi
                for hh in range(2):
                    v = self.wslot[sv[hh]].rearrange("p (kc col) -> p kc col", kc=KC)
                    pb = cnt["pb"] % 4
                    cnt["pb"] += 1
                    Z = bk[pb][:, 0:512]
                    k.mm([(Z, hT[:, c, bi * 128:(bi + 1) * 128], v[:, c, :], c == 0, c == KC - 1) for c in range(KC)],
                         reads=[self.wbuf[sv[hh]], b_hT], writes=[bb[pb]])
                    k.op(k.ACT, lambda: nc.scalar.activation(out=vg[vi][:, hh * 512:(hh + 1) * 512], in_=Z,
                                                             func=AF.Gelu_apprx_tanh),
                         reads=[bb[pb]], writes=[b_vg[vi]])

            def sg_s1(it):
                bi, vi = it["bi"], it["vi"]
                for hh in range(2):
                    k.op(k.DVE, lambda hh=hh: nc.vector.bn_stats(out=stats[:, hh * 6:(hh + 1) * 6],
                                                                in_=vg[vi][:, hh * 512:(hh + 1) * 512]),
                         reads=[b_vg[vi]], writes=[b_st])
                k.op(k.DVE, lambda: nc.vector.bn_aggr(out=mv, in_=stats), reads=[b_st], writes=[b_st])
                k.op(k.ACT, lambda: nc.scalar.activation(out=rstd, in_=mv[:, 1:2], func=AF.Sqrt, bias=self.eps_t, scale=1.0),
                     reads=[b_st, self.cbuf], writes=[b_st])
                k.op(k.DVE, lambda: nc.vector.reciprocal(out=rstd, in_=rstd), reads=[b_st], writes=[b_st])
                k.op(k.DVE, lambda: nc.vector.tensor_scalar(out=vg[vi], in0=vg[vi], scalar1=mv[:, 0:1], scalar2=rstd,
                                                           op0=ALU.subtract, op1=ALU.mult),
                     reads=[b_vg[vi], b_st], writes=[b_vg[vi]])
                k.op(k.DVE, lambda: nc.vector.tensor_tensor(out=vg[vi], in0=vg[vi], in1=lng, op=ALU.mult),
                     reads=[b_vg[vi], cb], writes=[b_vg[vi]])
                k.op(k.DVE, lambda: nc.vector.tensor_tensor(out=vn[:, bi, :], in0=vg[vi], in1=lnb, op=ALU.add),
                     reads=[b_vg[vi], cb], writes=[b_vn])

            def sg_s2(it):
                bi = it["bi"]
                for hh in range(2):
                    pb = 4 + cnt["g"] % 2
                    for g4 in range(4):
                        g = hh * 4 + g4
                        k.mm([(bk[pb][:, g4 * 128:(g4 + 1) * 128], vn[:, bi, g * 128:(g + 1) * 128], WmT[:, g, :], True, True)],
                             reads=[b_vn, cb], writes=[bb[pb]])
                    gi_ = cnt["g"] % 2
                    cnt["g"] += 1
                    gv = gt[gi_].rearrange("p (a b) -> p a b", a=4)
                    k.op(k.DVE, lambda: nc.vector.tensor_tensor(out=gv, in0=bk[pb].rearrange("p (a b) -> p a b", a=4),
                                                               in1=bsb[:, hh * 4:(hh + 1) * 4, :], op=ALU.add),
                         reads=[bb[pb], cb], writes=[b_gt[gi_]])
                    k.op(k.DVE, lambda: nc.vector.tensor_tensor(
                        out=mixT[:, 8 + hh * 4:8 + (hh + 1) * 4, bi * 128:(bi + 1) * 128], in0=gv,
                        in1=uT[:, hh * 4:(hh + 1) * 4, bi * 128:(bi + 1) * 128], op=ALU.mult),
                        reads=[b_gt[gi_], b_uT], writes=[b_mix])
            pipeline([dict(bi=bi, b=b) for bi, b in enumerate(sb)], [sg_s0, sg_s1, sg_s2])
            aitems = []
            for bi, b in enumerate(sb):
                if b < out_blocks[0]:
                    continue
                for g in range(2):
                    for par in range(2):
                        aitems.append(dict(bi=bi, b=b, g=g, par=par))

            def at_s0(it):
                bi, b, g, par = it["bi"], it["b"], it["g"], it["par"]
                a = cnt["a"] % 2
                cnt["a"] += 1
                it["a"] = a
                base = a * 4
                ps = slice(par * 64, (par + 1) * 64)
                mP = maskP2 if b == 2 else maskP
                rq = qr[ps, g * 4:(g + 1) * 4, bi * 128:(bi + 1) * 128]
                for kb, (kblk, msk) in enumerate(((b - 1, mP), (b, maskC))):
                    k.mm([(bk[base + kb], kr[ps, g, kblk * 128:(kblk + 1) * 128], rq, True, False),
                          (bk[base + kb], ident, msk.rearrange("p a b -> p (a b)"), False, True)],
                         reads=[b_kr, b_qr, cb], writes=[bb[base + kb]])
                    k.op(k.ACT, lambda kb=kb: nc.scalar.activation(out=ET[a][:, kb, :], in_=bk[base + kb],
                                                                 func=AF.Exp, scale=0.125),
                         reads=[bb[base + kb]], writes=[b_ET[a]])

            def at_s1(it):
                b, g, par, a = it["b"], it["g"], it["par"], it["a"]
                base = a * 4
                k.mm([(bk[base + 2], self.ones, ET[a][:, 0, :], True, False),
                      (bk[base + 2], self.ones, ET[a][:, 1, :], False, True)],
                     reads=[b_ET[a], self.cbuf], writes=[bb[base + 2]])
                k.mm([(bk[base + 3], vdup[:, b - 1, g * 128:(g + 1) * 128], ET[a][:, 0, :], True, False),
                      (bk[base + 3], vdup[:, b, g * 128:(g + 1) * 128], ET[a][:, 1, :], False, True)],
                     reads=[b_ET[a], b_vdup], writes=[bb[base + 3]])
                hs = (g * 2 + par) * 4
                k.op(k.DVE, lambda: nc.vector.tensor_tensor(
                    out=dn[a].rearrange("p (a b) -> p a b", a=4), in0=bk[base + 2].rearrange("p (a b) -> p a b", a=4),
                    in1=esink[:, hs:hs + 4].unsqueeze(2).to_broadcast([128, 4, 128]), op=ALU.add),
                    reads=[bb[base + 2], cb], writes=[b_dn[a]])

            def at_s2(it):
                bi, g, par, a = it["bi"], it["g"], it["par"], it["a"]
                base = a * 4
                ps = slice(par * 64, (par + 1) * 64)
                k.op(k.ACT, lambda: nc.scalar.activation(out=dn[a], in_=dn[a], func=AF.Ln),
                     reads=[b_dn[a]], writes=[b_dn[a]])
                k.op(k.ACT, lambda: nc.scalar.activation(out=rden[a], in_=dn[a], func=AF.Exp, scale=-1.0),
                     reads=[b_dn[a]], writes=[b_rden[a]])
                k.op(k.DVE, lambda: nc.vector.tensor_tensor(
                    out=mixT[ps, g * 4:(g + 1) * 4, bi * 128:(bi + 1) * 128],
                    in0=bk[base + 3][ps, :].rearrange("p (a b) -> p a b", a=4),
                    in1=rden[a][ps, :].rearrange("p (a b) -> p a b", a=4), op=ALU.mult),
                    reads=[bb[base + 3], b_rden[a]], writes=[b_mix])
            pipeline(aitems, [at_s0, at_s1, at_s2])
            osb = [b for b in sb if b >= out_blocks[0]]
            skip = len(sb) - len(osb)
            assert skip in (0, 1)
            if skip:
                inT = mixT[:, :, 128:]
            else:
                inT = mixT

            def wp(sl, ocs):
                v = sl.rearrange("p (kc col) -> p kc col", kc=KC)
                return [(v[:, :, 0:len(ocs) * 128], wout[:, :, ocs[0] * 128:(ocs[-1] + 1) * 128])]

            def ml(sl, oc, oi):
                v = sl.rearrange("p (kc col) -> p kc col", kc=KC)
                return [(v[:, c, oi * 128:(oi + 1) * 128], c) for c in range(KC)]
            self.proj_tail(osb, inT, [b_mix, b_mix], chunks_of(list(range(KC)), 4), wp, ml, l, 3, 1.0, Xin, Xout, ts,
                           interleave=(pre(supers[si + 1]) if si + 1 < len(supers) else None))
        k.barrier()
        k.top = mark

    def cross(self, l, Xin, Xout, blocks, xout_index=None):
        k = self.k
        nc = k.nc
        bk, bb = k.banks, k.bankbuf
        dr = k.dram
        mark = k.top
        wq = dr["x_wq"][l].rearrange("(c p) f -> p c f", p=128)
        wk = dr["x_wk"][l].rearrange("(c p) f -> p c f", p=128)
        wv = dr["x_wv"][l].rearrange("(c p) f -> p c f", p=128)
        wo = dr["x_wo"][l].rearrange("(c p) f -> p c f", p=128)
        memf = k.alloc([128, KC, 256], F32)
        memn = k.alloc([128, KC, 256], BF16)
        msq = k.alloc([128, KC, 256], BF16)
        msr = k.alloc([128, 256], F32)
        mg = k.alloc([128, KC], F32)
        KT = k.alloc([128, 4, 256], BF16)
        Vm = k.alloc([128, 2, 512], BF16)
        cb = self.gb[0]
        b_m = Buf()
        k.dma(k.SP, [(memf, dr["memT"]), (mg, dr["memnorm_t"][:, l * KC:(l + 1) * KC])], cb, writes=[cb])
        k.op(k.ACT, lambda: nc.scalar.activation(out=msq, in_=memf, func=AF.Square), reads=[cb], writes=[b_m])
        k.mm([(bk[7][:, 0:256], self.ones, msq[:, c, :], c == 0, c == KC - 1) for c in range(KC)],
             reads=[b_m, self.cbuf], writes=[bb[7]])
        k.op(k.ACT, lambda: nc.scalar.activation(out=msr, in_=bk[7][:, 0:256], func=AF.Sqrt, bias=self.eps_t, scale=1.0 / D),
             reads=[bb[7], self.cbuf], writes=[b_m])
        k.op(k.DVE, lambda: nc.vector.reciprocal(out=msr, in_=msr), reads=[b_m], writes=[b_m])
        for c in range(KC):
            k.op(k.DVE, lambda c=c: nc.vector.scalar_tensor_tensor(out=memn[:, c, :], in0=memf[:, c, :], scalar=mg[:, c:c + 1],
                                                                  in1=msr, op0=ALU.mult, op1=ALU.mult),
                 reads=[cb, b_m], writes=[b_m])
        full = lambda w_: (lambda sl: [(sl.rearrange("p (kc col) -> p kc col", kc=KC), w_)])
        cur = self.wload(full(wk))
        nxt = self.wload(full(wv))
        v = self.wslot[cur].rearrange("p (kc col) -> p kc col", kc=KC)
        for h in range(4):
            pb = h % 4
            k.mm([(bk[pb][:, 0:256], v[:, c, h * 128:(h + 1) * 128], memn[:, c, :], c == 0, c == KC - 1) for c in range(KC)],
                 reads=[self.wbuf[cur], b_m], writes=[bb[pb]])
            k.op(k.ACT, lambda: nc.scalar.activation(out=KT[:, h, :], in_=bk[pb][:, 0:256], func=AF.Copy),
                 reads=[bb[pb]], writes=[b_m])
        cur = nxt
        v = self.wslot[cur].rearrange("p (kc col) -> p kc col", kc=KC)
        for m in range(2):
            pb = m
            k.mm([(bk[pb], memn[:, c, m * 128:(m + 1) * 128], v[:, c, :], c == 0, c == KC - 1) for c in range(KC)],
                 reads=[self.wbuf[cur], b_m], writes=[bb[pb]])
            k.op(k.ACT, lambda: nc.scalar.activation(out=Vm[:, m, :], in_=bk[pb], func=AF.Copy),
                 reads=[bb[pb]], writes=[b_m])
        hT = k.alloc([128, KC, 768], BF16)
        qT = k.alloc([128, 4, 768], BF16)
        oT = k.alloc([128, 4, 768], BF16)
        ET = [k.alloc([128, 2, 384], BF16) for _ in range(2)]
        dn = [k.alloc([128, 384], F32) for _ in range(2)]
        rden = [k.alloc([128, 384], F32) for _ in range(2)]
        ns = self.norm_scratch()
        ts = self.tail_scratch()
        b_hT, b_qT, b_oT = Buf(), Buf(), [Buf(), Buf()]
        b_ET, b_dn, b_rden = [Buf(), Buf()], [Buf(), Buf()], [Buf(), Buf()]
        cnt = {"a": 0}
        sc = 128.0 ** -0.5
        supers = chunks_of(blocks, 6)
        for bi, b in enumerate(supers[0]):
            self.prenorm_to(ns, Xin, b, l, 4, lambda c, bi=bi: hT[:, c, bi * 128:(bi + 1) * 128], b_hT)
        for si, sb in enumerate(supers):
            subs = self.subtiles(len(sb))
            cur = self.wload(full(wq))
            v = self.wslot[cur].rearrange("p (kc col) -> p kc col", kc=KC)
            for h in range(4):
                for n, (off, w) in enumerate(subs):
                    pb = (h * 2 + n) % 4
                    k.mm([(bk[pb][:, 0:w], v[:, c, h * 128:(h + 1) * 128], hT[:, c, off:off + w], c == 0, c == KC - 1)
                          for c in range(KC)], reads=[self.wbuf[cur], b_hT], writes=[bb[pb]])
                    k.op(k.ACT, lambda: nc.scalar.activation(out=qT[:, h, off:off + w], in_=bk[pb][:, 0:w], func=AF.Copy),
                         reads=[bb[pb]], writes=[b_qT])
            citems = [dict(h=h, n=n, off=off, w=w) for h in range(4) for n, (off, w) in enumerate(subs)]

            def ca_s0(it):
                h, off, w = it["h"], it["off"], it["w"]
                a = cnt["a"] % 2
                cnt["a"] += 1
                it["a"] = a
                base = a * 4
                for m in range(2):
                    k.mm([(bk[base + m][:, 0:w], KT[:, h, m * 128:(m + 1) * 128], qT[:, h, off:off + w], True, True)],
                         reads=[b_m, b_qT], writes=[bb[base + m]])
                    k.op(k.ACT, lambda m=m: nc.scalar.activation(out=ET[a][:, m, 0:w], in_=bk[base + m][:, 0:w],
                                                               func=AF.Exp, scale=sc),
                         reads=[bb[base + m]], writes=[b_ET[a]])

            def ca_s1(it):
                h, w, a = it["h"], it["w"], it["a"]
                base = a * 4
                k.mm([(bk[base + 2][:, 0:w], self.ones, ET[a][:, 0, 0:w], True, False),
                      (bk[base + 2][:, 0:w], self.ones, ET[a][:, 1, 0:w], False, True)],
                     reads=[b_ET[a], self.cbuf], writes=[bb[base + 2]])
                k.mm([(bk[base + 3][:, 0:w], Vm[:, 0, h * 128:(h + 1) * 128], ET[a][:, 0, 0:w], True, False),
                      (bk[base + 3][:, 0:w], Vm[:, 1, h * 128:(h + 1) * 128], ET[a][:, 1, 0:w], False, True)],
                     reads=[b_ET[a], b_m], writes=[bb[base + 3]])
                k.op(k.ACT, lambda: nc.scalar.activation(out=dn[a][:, 0:w], in_=bk[base + 2][:, 0:w], func=AF.Ln),
                     reads=[bb[base + 2]], writes=[b_dn[a]])

            def ca_s2(it):
                h, n, off, w, a = it["h"], it["n"], it["off"], it["w"], it["a"]
                base = a * 4
                k.op(k.ACT, lambda: nc.scalar.activation(out=rden[a][:, 0:w], in_=dn[a][:, 0:w], func=AF.Exp, scale=-1.0),
                     reads=[b_dn[a]], writes=[b_rden[a]])
                k.op(k.DVE, lambda: nc.vector.tensor_tensor(out=oT[:, h, off:off + w], in0=bk[base + 3][:, 0:w],
                                                           in1=rden[a][:, 0:w], op=ALU.mult),
                     reads=[bb[base + 3], b_rden[a]], writes=[b_oT[n]])
            pipeline(citems, [ca_s0, ca_s1, ca_s2])

            def wp(sl, ocs):
                vv = sl.rearrange("p (kc col) -> p kc col", kc=4)
                return [(vv, wo)]

            def ml(sl, oc, oi):
                vv = sl.rearrange("p (kc col) -> p kc col", kc=4)
                return [(vv[:, c, oc * 128:(oc + 1) * 128], c) for c in range(4)]
            inter = []
            if si + 1 < len(supers):
                for bi, b in enumerate(supers[si + 1]):
                    inter.append(lambda bi=bi, b=b: self.prenorm_to(
                        ns, Xin, b, l, 4, lambda c, bi=bi: hT[:, c, bi * 128:(bi + 1) * 128], b_hT))
            self.proj_tail(sb, oT, b_oT, [list(range(KC))], wp, ml, l, 5, 1.0, Xin, Xout, ts, xout_index=xout_index,
                           interleave=inter)
        k.barrier()
        k.top = mark

    def pool1(self, Xin, Xout, out_blocks):
        k = self.k
        nc = k.nc
        dr = k.dram
        l = 1
        mark = k.top
        pw = dr["pool_w"][0].rearrange("g (cc p) d -> p g cc d", p=128)
        psc = k.alloc([128, KC], F32)
        flag = k.alloc([128, 1], F32)
        tab = k.alloc([128, 4, 16], F32)
        cb = self.gb[0]
        k.dma(k.SP, [(psc, dr["poolscale_t"]), (flag, dr["poolflag"]), (tab, dr["pooltab"])], cb, writes=[cb])
        hf = k.alloc([128, KC, 896], F32)
        ta = [k.alloc([128, 896], F32) for _ in range(2)]
        pooled = k.alloc([128, KC, 768], BF16)
        fx = k.alloc([128, 16], F32)
        ns = self.norm_scratch()
        ts = self.tail_scratch()
        b_hf, b_ta, b_pool, b_fx = Buf(), [Buf(), Buf()], [Buf(), Buf()], Buf()
        for i in range(2):
            k.op(k.DVE, lambda i=i: nc.vector.memset(ta[i], 0.0), writes=[b_ta[i]])
        psupers = chunks_of(out_blocks, 6)

        def pre(sb_):
            first_ = (sb_[0] == 2)
            return [lambda bi=bi, b=b: self.prenorm_to(
                ns, Xin, b, l, 2, lambda c, bi=bi: hf[:, c, bi * 128:(bi + 1) * 128], b_hf,
                flag=(flag if (first_ and bi == 0) else None), flag_buf=cb)
                for bi, b in enumerate([sb_[0] - 1] + sb_)]
        for f_ in pre(psupers[0]):
            f_()
        for si, sb in enumerate(psupers):
            nsb = len(sb)
            W = (nsb + 1) * 128
            first = (sb[0] == 2)
            for c in range(KC):
                gi = c // 4
                src = hf[:, c, :]
                sbuf = b_hf
                sh = 1
                for lev in range(gi + 1):
                    d = ta[lev % 2]
                    k.op(k.DVE, lambda: nc.vector.tensor_tensor(out=d[:, sh:W], in0=src[:, sh:W], in1=src[:, 0:W - sh],
                                                               op=ALU.add),
                         reads=[sbuf], writes=[b_ta[lev % 2]])
                    src, sbuf = d, b_ta[lev % 2]
                    sh *= 2
                wnd = 2 ** (gi + 1)
                k.op(k.DVE, lambda: nc.vector.scalar_tensor_tensor(out=pooled[:, c, 0:nsb * 128], in0=src[:, 128:W],
                                                                  scalar=1.0 / wnd, in1=hf[:, c, 128:W],
                                                                  op0=ALU.mult, op1=ALU.subtract),
                     reads=[sbuf, b_hf], writes=[b_pool[0]])
                if first:
                    k.op(k.DVE, lambda: nc.vector.tensor_tensor(out=fx, in0=src[:, 128:144], in1=tab[:, gi, :], op=ALU.mult),
                         reads=[sbuf, cb], writes=[b_fx])
                    k.op(k.DVE, lambda: nc.vector.tensor_tensor(out=pooled[:, c, 0:16], in0=fx, in1=hf[:, c, 128:144],
                                                               op=ALU.subtract),
                         reads=[b_fx, b_hf], writes=[b_pool[0]])

            def wp(sl, ocs):
                vv = sl.rearrange("p (g cc col) -> p g cc col", g=4, cc=4)
                return [(vv, pw)]

            def ml(sl, oc, oi):
                vv = sl.rearrange("p (g cc col) -> p g cc col", g=4, cc=4)
                g = oc // 4
                return [(vv[:, g, cc, (oc % 4) * 128:(oc % 4 + 1) * 128], g * 4 + cc) for cc in range(4)]
            self.proj_tail(sb, pooled, [b_pool[0], b_pool[0]], [list(range(KC))], wp, ml, l, 3, 1.0, Xin, Xout, ts,
                           scale_of=lambda oc: psc[:, oc:oc + 1], xreads=[cb],
                           interleave=(pre(psupers[si + 1]) if si + 1 < len(psupers) else None))
        k.barrier()
        k.top = mark


def x_layout(xc):
    nb = xc.shape[0] // 128
    return np.ascontiguousarray(xc.reshape(nb, 128, KC, 128).transpose(0, 3, 2, 1))


def x_unlayout(xl):
    nb = xl.shape[0]
    return np.ascontiguousarray(xl.transpose(0, 3, 2, 1).reshape(nb * 128, D))


def build_full(nblk=NBLK, dff=DFF):
    p = Prog(nblk=nblk, dff=dff)
    tok = nblk * 128
    X0 = p.din("x0", [nblk, 128, KC, 128])
    p.din("memT", [128, KC, MEM])
    p.din("norms_t", [128, 256])
    p.din("memnorm_t", [128, 32])
    p.din("poolscale_t", [128, 16])
    for w in (1, 2):
        p.din(f"ffn{w}_wg", [2, D, dff]); p.din(f"ffn{w}_wu", [2, D, dff]); p.din(f"ffn{w}_wd", [2, dff, D])
    p.din("x_wq", [2, D, 512]); p.din("x_wk", [2, D, 512]); p.din("x_wv", [2, D, 512]); p.din("x_wo", [2, 512, D])
    p.din("mix_w_in", [1, D, 3328]); p.din("mix_w_out", [1, D, D]); p.din("pool_w", [1, 4, 512, 512])
    p.din("sinks_b", [128, 16]); p.din("lng_b", [128, 1024]); p.din("lnb_b", [128, 1024])
    p.din("sgu_wT", [128, 8, 128]); p.din("bs_b", [128, 8, 128]); p.din("tril", [128, 128])
    p.din("ropeC", [128, tok]); p.din("ropeS", [128, tok]); p.din("ropeP", [128, 128]); p.din("ident", [128, 128])
    p.din("maskC", [128, 128]); p.din("maskP", [128, 128]); p.din("maskP2", [128, 128])
    p.din("poolflag", [128, 1]); p.din("pooltab", [128, 4, 16])
    OUT = p.dout("out", [nblk - 2, 128, KC, 128])
    XA = p.dscratch("xa", [nblk, 128, KC, 128])
    XB = p.dscratch("xb", [nblk, 128, KC, 128])
    allb = list(range(nblk))
    b1 = list(range(1, nblk))
    b2 = list(range(2, nblk))
    p.load_consts()
    p.ffn(0, 1, X0, XA, allb)
    p.mixer0(XA, XB, b1)
    p.cross(0, XB, XA, b1)
    p.ffn(0, 2, XA, XB, b1)
    p.ffn(1, 1, XB, XA, b1)
    p.pool1(XA, XB, b2)
    p.cross(1, XB, XA, b2)
    p.ffn(1, 2, XA, OUT, b2, xout_index=lambda b: b - 2)
    p.k.finish()
    return p


def host_consts(nblk, pos0, seq_start):
    tok = nblk * 128
    pos = (np.arange(tok) + pos0).astype(np.float32)
    half = 8
    inv = (np.float32(500000.0) ** (-np.arange(half, dtype=np.float32) * np.float32(2.0) / np.float32(16))).astype(np.float32)
    ang = (pos[:, None] * inv[None, :]).astype(np.float32)
    cos = np.cos(ang).astype(np.float32).T
    sin = np.sin(ang).astype(np.float32).T
    C = np.ones((128, tok), np.float32)
    S = np.zeros((128, tok), np.float32)
    PT = np.zeros((128, 128), np.float32)
    for p_ in range(128):
        d = p_ % 64
        if d < 8:
            C[p_] = cos[d]; S[p_] = -sin[d]; PT[p_ + 8, p_] = 1.0
        elif d < 16:
            C[p_] = cos[d - 8]; S[p_] = sin[d - 8]; PT[p_ - 8, p_] = 1.0
    j = np.arange(128)[:, None]
    i = np.arange(128)[None, :]
    NEG = np.float32(-30000.0)
    maskC = np.where(j <= i, np.float32(0), NEG).astype(np.float32)
    maskP = np.where(j > i, np.float32(0), NEG).astype(np.float32)
    maskP2 = np.full((128, 128), NEG, np.float32) if seq_start else maskP.copy()
    tril = (j <= i).astype(np.float32)
    flag = np.full((128, 1), 0.0 if seq_start else 1.0, np.float32)
    tab = np.zeros((128, 4, 16), np.float32)
    for gi, w in enumerate((2, 4, 8, 16)):
        for t in range(16):
            cnt = min(t + 1, w) if seq_start else w
            tab[:, gi, t] = np.float32(1.0) / np.float32(cnt)
    return dict(ropeC=C, ropeS=S, ropeP=PT, ident=np.eye(128, dtype=np.float32), maskC=maskC, maskP=maskP,
                maskP2=maskP2, tril=tril, poolflag=flag, pooltab=tab)


def host_shared(inp):
    f = lambda a: np.ascontiguousarray(np.asarray(a, dtype=np.float32))
    sh = {}
    norms = f(inp["norms"])
    sh["norms_t"] = f(norms.reshape(2, 8, KC, 128).transpose(3, 0, 1, 2).reshape(128, 256))
    sh["memnorm_t"] = f(f(inp["mem_norm"]).reshape(2, KC, 128).transpose(2, 0, 1).reshape(128, 32))
    sh["poolscale_t"] = f(f(inp["pool_scale"])[0].reshape(KC, 128).T)
    for nm in ("ffn1_wg", "ffn1_wu", "ffn1_wd", "ffn2_wg", "ffn2_wu", "ffn2_wd", "x_wq", "x_wk", "x_wv", "x_wo",
               "mix_w_in", "mix_w_out", "pool_w"):
        sh[nm] = f(inp[nm])
    sinks = f(inp["attn_sinks"])[0]
    perm = np.zeros(16, np.float32)
    for g in range(2):
        for par in range(2):
            for i in range(4):
                perm[(g * 2 + par) * 4 + i] = sinks[2 * (g * 4 + i) + par]
    sh["sinks_b"] = f(np.broadcast_to(perm[None, :], (128, 16)))
    sh["lng_b"] = f(np.broadcast_to(f(inp["sgu_ln_g"])[0][None, :], (128, 1024)))
    sh["lnb_b"] = f(np.broadcast_to(f(inp["sgu_ln_b"])[0][None, :], (128, 1024)))
    sh["sgu_wT"] = f(f(inp["sgu_w"])[0].transpose(2, 0, 1))
    sh["bs_b"] = f(np.broadcast_to(f(inp["sgu_b"])[0][None, :, :], (128, 8, 128)))
    return sh


def core_inputs(inp, sh, c, nblk=NBLK, seq=SEQ):
    own = (nblk - 2) * 128
    per_seq = seq // own
    b, hf = c // per_seq, c % per_seq
    x = np.asarray(inp["x"], dtype=np.float32)
    start = hf * own - 256
    xc = np.zeros((nblk * 128, D), np.float32)
    lo = max(start, 0)
    xc[lo - start:] = x[b, lo:start + nblk * 128]
    m = {"x0": x_layout(xc)}
    mem = np.asarray(inp["mem"], dtype=np.float32)[b]
    m["memT"] = np.ascontiguousarray(mem.reshape(MEM, KC, 128).transpose(2, 1, 0))
    m.update(sh)
    m.update(host_consts(nblk, start, hf == 0))
    return m


def kernel(**inputs):
    p = build_full()
    sh = host_shared(inputs)
    in_maps = [core_inputs(inputs, sh, c) for c in range(NCORES)]
    res = run_bass_kernel_spmd(p.k.nc, in_maps, core_ids=list(range(NCORES)))
    out = np.zeros((BATCH, SEQ, D), np.float32)
    for c in range(NCORES):
        b, hf = c // 2, c % 2
        out[b, hf * HALF:(hf + 1) * HALF] = x_unlayout(np.asarray(res.results[c]["out"]))
    return out
```

```python
import numpy as np
import concourse.bass as bass
import concourse.mybir as mybir
from concourse.bass_utils import run_bass_kernel_spmd

F32 = mybir.dt.float32
BF16 = mybir.dt.bfloat16
AF = mybir.ActivationFunctionType
ALU = mybir.AluOpType
AX = mybir.AxisListType

D = 2048
KC = 16
DFF = 5632
NCORES = 8
SEQ = 4096
BATCH = 4
HALF = SEQ // 2
NBLK = 18
TOK = NBLK * 128
EPS = 1e-6
MEM = 256
SAME_RAW_SYNC = True


def _dtsize(dt):
    return 2 if dt == BF16 else 4


class Buf:
    __slots__ = ("w", "r", "dsem", "dcount", "name")

    def __init__(self, name=""):
        self.w = {}
        self.r = {}
        self.dsem = None
        self.dcount = 0
        self.name = name


class Eng:
    def __init__(self, K, eng, name, counter=True):
        self.K = K
        self.eng = eng
        self.name = name
        self.sem = K.newsem("c_" + name) if counter else None
        self.count = 0
        self.waited = {}

    def wait_all(self, evs):
        best = {}
        for ev in evs:
            key = id(ev[0])
            if key not in best or best[key][1] < ev[1]:
                best[key] = ev
        for key, ev in best.items():
            if self.waited.get(key, 0) >= ev[1]:
                continue
            self.eng.wait_ge(ev[0], ev[1])
            self.waited[key] = ev[1]


class K:
    def __init__(self, arena_words=53200):
        self.nc = bass.Bass("TRN2", target_bir_lowering=False)
        nc = self.nc
        self.sems = []
        self.PE = Eng(self, nc.tensor, "pe")
        self.ACT = Eng(self, nc.scalar, "act")
        self.DVE = Eng(self, nc.vector, "dve")
        self.POOL = Eng(self, nc.gpsimd, "pool")
        self.SP = Eng(self, nc.sync, "sp", counter=False)
        self.engs = [self.PE, self.ACT, self.DVE, self.POOL, self.SP]
        self.arena = nc.sbuf_tensor("arena", [128, arena_words], F32).__enter__()
        self.arena_words = arena_words
        self.top = 0
        self.bank_h = [nc.psum_tensor(f"bank{i}", [128, 512], F32).__enter__() for i in range(8)]
        self.banks = [h[:, :] for h in self.bank_h]
        self.bankbuf = [Buf(f"bank{i}") for i in range(8)]
        self.dma_bufs = []
        self.dram = {}

    def newsem(self, name):
        s = self.nc.semaphore(name).__enter__()
        self.sems.append(s)
        return s

    def alloc(self, shape, dt):
        n = 1
        for s in shape[1:]:
            n *= s
        words = (n * _dtsize(dt) + 3) // 4
        words = (words + 7) // 8 * 8
        off = self.top
        self.top += words
        assert self.top <= self.arena_words, f"arena overflow {self.top}"
        ap = self.arena[:, off:off + words]
        if dt != F32:
            ap = ap.bitcast(dt)
        ap = ap[:, 0:n]
        if len(shape) == 3:
            ap = ap.rearrange("p (a b) -> p a b", a=shape[1])
        elif len(shape) == 4:
            ap = ap.rearrange("p (a b c) -> p a b c", a=shape[1], b=shape[2])
        return ap

    def dmabuf(self, name):
        b = Buf(name)
        b.dsem = self.newsem("d_" + name)
        self.dma_bufs.append(b)
        return b

    def _deps(self, E, reads, writes, extra, nosame=False):
        deps = list(extra)
        for b in reads:
            for ev in b.w.values():
                if nosame and ev[2] is E:
                    continue
                deps.append(ev)
        for b in writes:
            for ev in list(b.w.values()) + list(b.r.values()):
                if nosame and ev[2] is E:
                    continue
                deps.append(ev)
        return deps

    @staticmethod
    def _mark(ev, reads, writes):
        key = id(ev[0])
        for b in reads:
            b.r[key] = ev
        for b in writes:
            b.w[key] = ev

    LIMIT = None
    ncalls = 0

    def _skip(self):
        K.ncalls += 1
        return K.LIMIT is not None and K.ncalls > K.LIMIT

    def op(self, E, fn, reads=(), writes=(), extra=(), nosame=False):
        if self._skip():
            return None
        E.wait_all(self._deps(E, reads, writes, extra, nosame))
        ins = fn()
        ins.then_inc(E.sem, 1)
        E.count += 1
        ev = (E.sem, E.count, E)
        self._mark(ev, reads, writes)
        return ev

    def mm(self, mms, reads=(), writes=(), extra=()):
        E = self.PE
        if self._skip():
            return None
        E.wait_all(self._deps(E, reads, writes, extra))
        ins = None
        for (o, l, r, st, sp) in mms:
            ins = self.nc.tensor.matmul(o, lhsT=l, rhs=r, start=st, stop=sp)
        ins.then_inc(E.sem, 1)
        E.count += 1
        ev = (E.sem, E.count, E)
        self._mark(ev, reads, writes)
        return ev

    def dma(self, Q, pairs, dbuf, reads=(), writes=(), extra=()):
        if self._skip():
            return None
        Q.wait_all(self._deps(Q, reads, writes, extra))
        for (o, i) in pairs:
            Q.eng.dma_start(out=o, in_=i).then_inc(dbuf.dsem, 16)
            dbuf.dcount += 16
        ev = (dbuf.dsem, dbuf.dcount, None)
        self._mark(ev, reads, writes)
        return ev

    def barrier(self, bufs=()):
        evs = []
        for E in self.engs:
            if E.sem is not None and E.count > 0:
                evs.append((E.sem, E.count, E))
        for b in self.dma_bufs:
            if b.dcount > 0:
                evs.append((b.dsem, b.dcount, None))
        for E in self.engs:
            E.wait_all([ev for ev in evs if ev[2] is not E])

    def finish(self):
        self.barrier()


def chunks_of(lst, n):
    return [lst[i:i + n] for i in range(0, len(lst), n)]


def pipeline(items, stages):
    n, d = len(items), len(stages)
    for t in range(n + d - 1):
        for s_ in range(d):
            i = t - s_
            if 0 <= i < n:
                stages[s_](items[i])


class Prog:
    def __init__(self, nblk=NBLK, dff=DFF, debug_out=None):
        self.k = K()
        k = self.k
        nc = k.nc
        self.nblk = nblk
        self.dff = dff
        self.fc = dff // 128
        self.ones = k.alloc([128, 128], BF16)
        self.gains = k.alloc([128, 2 * 8 * 16], F32)
        self.eps_t = k.alloc([128, 1], F32)
        self.wslot = [k.alloc([128, 8192], BF16) for _ in range(2)]
        self.wbuf = [k.dmabuf(f"w{i}") for i in range(2)]
        self.wstep = 0
        self.cbuf = k.dmabuf("consts")
        self.xblk = [k.alloc([128, KC, 128], F32) for _ in range(2)]
        self.b_x = [k.dmabuf("xblk0"), k.dmabuf("xblk1")]
        self.xcnt = 0
        self.gb = [k.dmabuf(f"g{i}") for i in range(4)]
        self.persist_top = k.top

    def din(self, name, shape, dt=F32):
        t = self.k.nc.dram_tensor(name, list(shape), dt, kind="ExternalInput").ap()
        self.k.dram[name] = t
        return t

    def dout(self, name, shape, dt=F32):
        t = self.k.nc.dram_tensor(name, list(shape), dt, kind="ExternalOutput").ap()
        self.k.dram[name] = t
        return t

    def dscratch(self, name, shape, dt=F32):
        t = self.k.nc.dram_tensor(name, list(shape), dt, kind="Internal").ap()
        self.k.dram[name] = t
        return t

    def load_consts(self):
        k = self.k
        nc = k.nc
        k.op(k.DVE, lambda: nc.vector.memset(self.ones, 1.0), writes=[self.cbuf])
        k.op(k.DVE, lambda: nc.vector.memset(self.eps_t, EPS), writes=[self.cbuf])
        k.dma(k.SP, [(self.gains, k.dram["norms_t"])], self.cbuf, writes=[self.cbuf])

    def gain(self, l, n, kc):
        c = (l * 8 + n) * 16 + kc
        return self.gains[:, c:c + 1]

    def wload(self, pairs_fn):
        k = self.k
        s = self.wstep % 2
        self.wstep += 1
        k.dma(k.POOL, pairs_fn(self.wslot[s]), self.wbuf[s], writes=[self.wbuf[s]])
        return s

    def ffn(self, l, which, Xin, Xout, blocks, xout_index=None):
        k = self.k
        nc = k.nc
        fc = self.fc
        wg = k.dram[f"ffn{which}_wg"][l].rearrange("(c p) f -> p c f", p=128)
        wu = k.dram[f"ffn{which}_wu"][l].rearrange("(c p) f -> p c f", p=128)
        wd = k.dram[f"ffn{which}_wd"][l].rearrange("(j p) d -> p j d", p=128)
        npre, npost = (0, 1) if which == 1 else (6, 7)
        mark = k.top
        STB = 6
        STOK = STB * 128
        hT = k.alloc([128, KC, STOK], BF16)
        A = k.alloc([128, fc, STOK], BF16)
        Y = k.alloc([128, KC, STOK], F32)
        xblk = self.xblk
        sq = k.alloc([128, KC, 128], BF16)
        srt = k.alloc([128, 128], F32)
        Rpre = k.alloc([128, 128], F32)
        RY = k.alloc([128, STOK], F32)
        srtY = k.alloc([128, 384], F32)
        sg = [k.alloc([128, 384], BF16) for _ in range(2)]
        sqY = [k.alloc([128, 384], BF16) for _ in range(2)]
        b_hT = [Buf("hT0"), Buf("hT1")]
        b_A = [Buf("A0"), Buf("A1")]
        b_Y = [Buf("Y0"), Buf("Y1")]
        b_x = self.b_x
        b_sq, b_srt, b_R, b_RY, b_srtY, b_tmp = Buf(), Buf(), Buf(), Buf(), Buf(), Buf()
        b_sg = [Buf(), Buf()]
        b_sqY = [Buf(), Buf()]
        b_ss = [Buf() for _ in range(4)]
        bk = k.banks
        bb = k.bankbuf
        st = {"x": 0, "sg": 0, "sqY": 0, "pb": 0, "nb": 0}

        supers = chunks_of(blocks, STB)

        def subtiles(sblocks):
            res = []
            for i in range(0, len(sblocks), 3):
                w = min(3, len(sblocks) - i) * 128
                res.append((i * 128, w))
            return res

        def prenorm_block(sblocks, bi):
            b = sblocks[bi]
            xs = st["x"] % 2
            st["x"] += 1
            k.dma(k.SP, [(xblk[xs], Xin[b])], b_x[xs], writes=[b_x[xs]])
            k.op(k.ACT, lambda: nc.scalar.activation(out=sq, in_=xblk[xs], func=AF.Square),
                 reads=[b_x[xs]], writes=[b_sq])
            nb = st["nb"] % 4
            st["nb"] += 1
            ssr = bk[6][:, nb * 128:(nb + 1) * 128]
            k.mm([(ssr, self.ones, sq[:, c, :], c == 0, c == KC - 1) for c in range(KC)],
                 reads=[b_sq, self.cbuf], writes=[b_ss[nb]])
            k.op(k.ACT, lambda: nc.scalar.activation(out=srt, in_=ssr, func=AF.Sqrt,
                                                     bias=self.eps_t, scale=1.0 / D),
                 reads=[b_ss[nb], self.cbuf], writes=[b_srt])
            k.op(k.DVE, lambda: nc.vector.reciprocal(out=Rpre, in_=srt), reads=[b_srt], writes=[b_R])
            n = bi // 3
            for c in range(KC):
                k.op(k.DVE, lambda c=c: nc.vector.scalar_tensor_tensor(
                    out=hT[:, c, bi * 128:(bi + 1) * 128], in0=xblk[xs][:, c, :],
                    scalar=self.gain(l, npre, c), in1=Rpre, op0=ALU.mult, op1=ALU.mult),
                    reads=[b_x[xs], b_R, self.cbuf], writes=[b_hT[n]], nosame=(c > 0))

        tail_state = {}

        def tail_a(sblocks, bi):
            b = sblocks[bi]
            n = bi // 3
            xs = st["x"] % 2
            st["x"] += 1
            tail_state[(id(sblocks), bi)] = xs
            k.dma(k.SP, [(xblk[xs], Xin[b])], b_x[xs], writes=[b_x[xs]])
            for c in range(0, KC // 2):
                k.op(k.DVE, lambda c=c: nc.vector.scalar_tensor_tensor(
                    out=Y[:, c, bi * 128:(bi + 1) * 128], in0=Y[:, c, bi * 128:(bi + 1) * 128],
                    scalar=self.gain(l, npost, c), in1=RY[:, bi * 128:(bi + 1) * 128],
                    op0=ALU.mult, op1=ALU.mult),
                    reads=[b_Y[n], b_RY, self.cbuf], writes=[b_Y[n]], nosame=(c > 0))

        def tail_b(sblocks, bi):
            b = sblocks[bi]
            n = bi // 3
            xs = tail_state.pop((id(sblocks), bi))
            for c in range(KC // 2, KC):
                k.op(k.DVE, lambda c=c: nc.vector.scalar_tensor_tensor(
                    out=Y[:, c, bi * 128:(bi + 1) * 128], in0=Y[:, c, bi * 128:(bi + 1) * 128],
                    scalar=self.gain(l, npost, c), in1=RY[:, bi * 128:(bi + 1) * 128],
                    op0=ALU.mult, op1=ALU.mult),
                    reads=[b_Y[n], b_RY, self.cbuf], writes=[b_Y[n]], nosame=True)
            k.op(k.DVE, lambda: nc.vector.scalar_tensor_tensor(
                out=xblk[xs], in0=Y[:, :, bi * 128:(bi + 1) * 128], scalar=0.5, in1=xblk[xs],
                op0=ALU.mult, op1=ALU.add),
                reads=[b_Y[n], b_x[xs]], writes=[b_x[xs]])
            k.dma(k.SP, [(Xout[b if xout_index is None else xout_index(b)], xblk[xs])], b_x[xs], reads=[b_x[xs]])

        def tail_block(sblocks, bi):
            tail_a(sblocks, bi)
            tail_b(sblocks, bi)

        steps = []
        for si, sb in enumerate(supers):
            for jj in range(fc // 2):
                steps.append(("gu", si, jj))
            for i in range(KC):
                steps.append(("d", si, i))

        def issue(step):
            kind, si, idx = step
            if kind == "gu":
                def pf(slot):
                    v = slot.rearrange("p (m c f) -> p m c f", m=2, c=KC)
                    return [(v[:, 0], wg[:, :, idx * 256:(idx + 1) * 256]),
                            (v[:, 1], wu[:, :, idx * 256:(idx + 1) * 256])]
                return self.wload(pf)
            else:
                def pf(slot):
                    v = slot[:, 0:fc * 128].rearrange("p (j d) -> p j d", j=fc)
                    return [(v, wd[:, :, idx * 128:(idx + 1) * 128])]
                return self.wload(pf)

        for bi in range(len(supers[0])):
            prenorm_block(supers[0], bi)
        slot_of = {0: issue(steps[0])}
        pend_ss = None
        for t, step in enumerate(steps):
            if t + 1 < len(steps):
                slot_of[t + 1] = issue(steps[t + 1])
            kind, si, idx = step
            sb = supers[si]
            subs = subtiles(sb)
            ws = slot_of[t]
            wv = self.wslot[ws]
            if kind == "gu":
                v = wv.rearrange("p (m c f) -> p m c f", m=2, c=KC)
                for jl in range(2):
                    j = idx * 2 + jl
                    for n, (off, w) in enumerate(subs):
                        pb = (st["pb"] % 2) * 2
                        st["pb"] += 1
                        G = bk[pb][:, 0:w]
                        U = bk[pb + 1][:, 0:w]
                        k.mm([(G, v[:, 0, c, jl * 128:(jl + 1) * 128], hT[:, c, off:off + w], c == 0, c == KC - 1)
                              for c in range(KC)], reads=[self.wbuf[ws], b_hT[n]], writes=[bb[pb]])
                        k.mm([(U, v[:, 1, c, jl * 128:(jl + 1) * 128], hT[:, c, off:off + w], c == 0, c == KC - 1)
                              for c in range(KC)], reads=[self.wbuf[ws], b_hT[n]], writes=[bb[pb + 1]])
                        ss = st["sg"] % 2
                        st["sg"] += 1
                        k.op(k.ACT, lambda: nc.scalar.activation(out=sg[ss][:, 0:w], in_=G, func=AF.Silu),
                             reads=[bb[pb]], writes=[b_sg[ss]])
                        k.op(k.DVE, lambda: nc.vector.tensor_tensor(out=A[:, j, off:off + w], in0=sg[ss][:, 0:w],
                                                                   in1=U, op=ALU.mult),
                             reads=[b_sg[ss], bb[pb + 1]], writes=[b_A[n]])
                    if si > 0 and idx < len(supers[si - 1]):
                        (tail_a if jl == 0 else tail_b)(supers[si - 1], idx)
                if si > 0 and idx == fc // 2 - 1:
                    for bi in range(fc // 2, len(supers[si - 1])):
                        tail_block(supers[si - 1], bi)
            else:
                i = idx
                v = wv[:, 0:fc * 128].rearrange("p (j d) -> p j d", j=fc)
                for n, (off, w) in enumerate(subs):
                    pb = st["pb"] % 4
                    st["pb"] += 1
                    Yp = bk[pb][:, 0:w]
                    k.mm([(Yp, v[:, j, :], A[:, j, off:off + w], j == 0, j == fc - 1) for j in range(fc)],
                         reads=[self.wbuf[ws], b_A[n]], writes=[bb[pb]])
                    if pend_ss is not None:
                        pi, pn, pw, pq = pend_ss
                        k.mm([(bk[4 + pn][:, 0:pw], self.ones, sqY[pq][:, 0:pw], pi == 0, pi == KC - 1)],
                             reads=[b_sqY[pq], self.cbuf], writes=[bb[4 + pn]])
                        pend_ss = None
                    k.op(k.DVE, lambda: nc.vector.tensor_copy(out=Y[:, i, off:off + w], in_=Yp),
                         reads=[bb[pb]], writes=[b_Y[n]])
                    q = st["sqY"] % 2
                    st["sqY"] += 1
                    k.op(k.ACT, lambda: nc.scalar.activation(out=sqY[q][:, 0:w], in_=Y[:, i, off:off + w], func=AF.Square),
                         reads=[b_Y[n]], writes=[b_sqY[q]])
                    pend_ss = (i, n, w, q)
                if si + 1 < len(supers) and i < len(supers[si + 1]):
                    prenorm_block(supers[si + 1], i)
                if i == KC - 1:
                    pi, pn, pw, pq = pend_ss
                    k.mm([(bk[4 + pn][:, 0:pw], self.ones, sqY[pq][:, 0:pw], pi == 0, pi == KC - 1)],
                         reads=[b_sqY[pq], self.cbuf], writes=[bb[4 + pn]])
                    pend_ss = None
                    for n, (off, w) in enumerate(subs):
                        k.op(k.ACT, lambda: nc.scalar.activation(out=srtY[:, 0:w], in_=bk[4 + n][:, 0:w], func=AF.Sqrt,
                                                                 bias=self.eps_t, scale=1.0 / D),
                             reads=[bb[4 + n], self.cbuf], writes=[b_srtY])
                        k.op(k.DVE, lambda: nc.vector.reciprocal(out=RY[:, off:off + w], in_=srtY[:, 0:w]),
                             reads=[b_srtY], writes=[b_RY])
                    if si == len(supers) - 1:
                        for bi in range(len(sb)):
                            tail_block(sb, bi)
        k.barrier()
        k.top = mark

    def norm_scratch(self):
        k = self.k
        ns = {}
        ns["sq"] = k.alloc([128, KC, 128], BF16)
        ns["srt"] = k.alloc([128, 128], F32)
        ns["R"] = k.alloc([128, 128], F32)
        ns["b_sq"], ns["b_srt"], ns["b_R"] = Buf(), Buf(), Buf()
        ns["b_ss"] = [Buf() for _ in range(4)]
        ns["nb"] = 0
        return ns

    def prenorm_to(self, ns, Xin, b, l, nidx, dst_of_chunk, dst_buf, flag=None, flag_buf=None):
        k = self.k
        nc = k.nc
        xs = self.xcnt % 2
        self.xcnt += 1
        xblk, b_x = self.xblk, self.b_x
        k.dma(k.SP, [(xblk[xs], Xin[b])], b_x[xs], writes=[b_x[xs]])
        k.op(k.ACT, lambda: nc.scalar.activation(out=ns["sq"], in_=xblk[xs], func=AF.Square),
             reads=[b_x[xs]], writes=[ns["b_sq"]])
        nb = ns["nb"] % 4
        ns["nb"] += 1
        ssr = k.banks[6][:, nb * 128:(nb + 1) * 128]
        k.mm([(ssr, self.ones, ns["sq"][:, c, :], c == 0, c == KC - 1) for c in range(KC)],
             reads=[ns["b_sq"], self.cbuf], writes=[ns["b_ss"][nb], k.bankbuf[6]])
        k.op(k.ACT, lambda: nc.scalar.activation(out=ns["srt"], in_=ssr, func=AF.Sqrt,
                                                 bias=self.eps_t, scale=1.0 / D),
             reads=[ns["b_ss"][nb], k.bankbuf[6], self.cbuf], writes=[ns["b_srt"]])
        k.op(k.DVE, lambda: nc.vector.reciprocal(out=ns["R"], in_=ns["srt"]),
             reads=[ns["b_srt"]], writes=[ns["b_R"]])
        if flag is not None:
            k.op(k.DVE, lambda: nc.vector.tensor_scalar(out=ns["R"], in0=ns["R"], scalar1=flag, scalar2=None,
                                                       op0=ALU.mult),
                 reads=[ns["b_R"], self.cbuf, flag_buf], writes=[ns["b_R"]])
        for c in range(KC):
            k.op(k.DVE, lambda c=c: nc.vector.scalar_tensor_tensor(
                out=dst_of_chunk(c), in0=xblk[xs][:, c, :],
                scalar=self.gain(l, nidx, c), in1=ns["R"], op0=ALU.mult, op1=ALU.mult),
                reads=[b_x[xs], ns["b_R"], self.cbuf], writes=[dst_buf], nosame=(c > 0))

    def tail_scratch(self, Y=None, ntok=768):
        k = self.k
        ts = {}
        ts["Y"] = Y if Y is not None else k.alloc([128, KC, 768], F32)
        ts["RY"] = k.alloc([128, ntok], F32)
        ts["srtY"] = k.alloc([128, 384], F32)
        ts["sqY"] = [k.alloc([128, 384], BF16) for _ in range(2)]
        ts["b_Y"] = [Buf(), Buf()]
        ts["b_RY"], ts["b_srtY"] = Buf(), Buf()
        ts["b_sqY"] = [Buf(), Buf()]
        ts["q"] = 0
        ts["pb"] = 0
        return ts

    @staticmethod
    def subtiles(nblocks):
        res = []
        for i in range(0, nblocks, 3):
            res.append((i * 128, min(3, nblocks - i) * 128))
        return res

    def proj_tail(self, sblocks, inT, b_in, groups, wpairs, mmlist, l, npost, factor, Xin, Xout, ts,
                  scale_of=None, xout_index=None, xreads=(), interleave=None, defer_tail=False):
        k = self.k
        nc = k.nc
        bk, bb = k.banks, k.bankbuf
        Y, RY, srtY, sqY = ts["Y"], ts["RY"], ts["srtY"], ts["sqY"]
        subs = self.subtiles(len(sblocks))
        xblk, b_x = self.xblk, self.b_x
        slot = self.wload(lambda sl: wpairs(sl, groups[0]))
        pend = None

        def flush():
            pi, pn, pw, pq = pend
            k.mm([(bk[4 + pn][:, 0:pw], self.ones, sqY[pq][:, 0:pw], pi == 0, pi == KC - 1)],
                 reads=[ts["b_sqY"][pq], self.cbuf], writes=[bb[4 + pn]])

        for gi, ocs in enumerate(groups):
            cur = slot
            if gi + 1 < len(groups):
                slot = self.wload(lambda sl: wpairs(sl, groups[gi + 1]))
            for oi, oc in enumerate(ocs):
                lst = mmlist(self.wslot[cur], oc, oi)
                for n, (off, w) in enumerate(subs):
                    pb = ts["pb"] % 4
                    ts["pb"] += 1
                    Yp = bk[pb][:, 0:w]
                    k.mm([(Yp, lh, inT[:, ic, off:off + w], j == 0, j == len(lst) - 1)
                          for j, (lh, ic) in enumerate(lst)],
                         reads=[self.wbuf[cur], b_in[n]], writes=[bb[pb]])
                    if pend is not None:
                        flush()
                        pend = None
                    if scale_of is None:
                        k.op(k.DVE, lambda: nc.vector.tensor_copy(out=Y[:, oc, off:off + w], in_=Yp),
                             reads=[bb[pb]], writes=[ts["b_Y"][n]])
                    else:
                        k.op(k.DVE, lambda: nc.vector.tensor_scalar(out=Y[:, oc, off:off + w], in0=Yp,
                                                                   scalar1=scale_of(oc), scalar2=None, op0=ALU.mult),
                             reads=[bb[pb], self.cbuf] + list(xreads), writes=[ts["b_Y"][n]])
                    q = ts["q"] % 2
                    ts["q"] += 1
                    k.op(k.ACT, lambda: nc.scalar.activation(out=sqY[q][:, 0:w], in_=Y[:, oc, off:off + w],
                                                             func=AF.Square),
                         reads=[ts["b_Y"][n]], writes=[ts["b_sqY"][q]])
                    pend = (oc, n, w, q)
                if interleave:
                    interleave.pop(0)()
        while interleave:
            interleave.pop(0)()
        flush()
        pend = None
        for n, (off, w) in enumerate(subs):
            k.op(k.ACT, lambda: nc.scalar.activation(out=srtY[:, 0:w], in_=bk[4 + n][:, 0:w], func=AF.Sqrt,
                                                     bias=self.eps_t, scale=1.0 / D),
                 reads=[bb[4 + n], self.cbuf], writes=[ts["b_srtY"]])
            k.op(k.DVE, lambda: nc.vector.reciprocal(out=RY[:, off:off + w], in_=srtY[:, 0:w]),
                 reads=[ts["b_srtY"]], writes=[ts["b_RY"]])
        closures = []
        for bi, b in enumerate(sblocks):
            stt = {}

            def ta(bi=bi, b=b, stt=stt):
                n = bi // 3
                xs = self.xcnt % 2
                self.xcnt += 1
                stt["xs"] = xs
                k.dma(k.SP, [(xblk[xs], Xin[b])], b_x[xs], writes=[b_x[xs]])
                for c in range(KC // 2):
                    k.op(k.DVE, lambda c=c: nc.vector.scalar_tensor_tensor(
                        out=Y[:, c, bi * 128:(bi + 1) * 128], in0=Y[:, c, bi * 128:(bi + 1) * 128],
                        scalar=self.gain(l, npost, c), in1=RY[:, bi * 128:(bi + 1) * 128],
                        op0=ALU.mult, op1=ALU.mult),
                        reads=[ts["b_Y"][n], ts["b_RY"], self.cbuf], writes=[ts["b_Y"][n]], nosame=(c > 0))

            def tb(bi=bi, b=b, stt=stt):
                n = bi // 3
                xs = stt["xs"]
                for c in range(KC // 2, KC):
                    k.op(k.DVE, lambda c=c: nc.vector.scalar_tensor_tensor(
                        out=Y[:, c, bi * 128:(bi + 1) * 128], in0=Y[:, c, bi * 128:(bi + 1) * 128],
                        scalar=self.gain(l, npost, c), in1=RY[:, bi * 128:(bi + 1) * 128],
                        op0=ALU.mult, op1=ALU.mult),
                        reads=[ts["b_Y"][n], ts["b_RY"], self.cbuf], writes=[ts["b_Y"][n]], nosame=True)
                k.op(k.DVE, lambda: nc.vector.scalar_tensor_tensor(
                    out=xblk[xs], in0=Y[:, :, bi * 128:(bi + 1) * 128], scalar=float(factor), in1=xblk[xs],
                    op0=ALU.mult, op1=ALU.add),
                    reads=[ts["b_Y"][n], b_x[xs]], writes=[b_x[xs]])
                ob = b if xout_index is None else xout_index(b)
                k.dma(k.SP, [(Xout[ob], xblk[xs])], b_x[xs], reads=[b_x[xs]])
            closures += [ta, tb]
        if defer_tail:
            return closures
        for f_ in closures:
            f_()
        return []

    def mixer0(self, Xin, Xout, out_blocks):
        k = self.k
        nc = k.nc
        bk, bb = k.banks, k.bankbuf
        l = 0
        mark = k.top
        dr = k.dram
        win = dr["mix_w_in"][0].rearrange("(c p) f -> p c f", p=128)
        wout = dr["mix_w_out"][0].rearrange("(c p) f -> p c f", p=128)
        nblk_all = self.nblk
        ropeP = k.alloc([128, 128], BF16)
        ident = k.alloc([128, 128], BF16)
        maskC = k.alloc([128, 4, 128], BF16)
        maskP = k.alloc([128, 4, 128], BF16)
        maskP2 = k.alloc([128, 4, 128], BF16)
        WmT = k.alloc([128, 8, 128], BF16)
        bsb = k.alloc([128, 8, 128], F32)
        lng = k.alloc([128, 1024], F32)
        lnb = k.alloc([128, 1024], F32)
        esink = k.alloc([128, 16], F32)
        kr = k.alloc([128, 2, nblk_all * 128], BF16)
        vdup = k.alloc([128, nblk_all, 256], BF16)
        b_kr, b_vdup = Buf(), Buf()
        STM = 4
        MT = STM * 128
        R0 = k.alloc([128, KC, MT], F32)
        r0b = R0.rearrange("p a b -> p (a b)")
        hT = k.alloc([128, KC, MT], BF16)
        uT = r0b[:, 8 * MT:16 * MT].rearrange("p (a b) -> p a b", a=8)
        wsf = r0b[:, 0:1024].rearrange("p (a b) -> p a b", a=8)
        trilf = r0b[:, 1024:1152]
        ropC = k.alloc([128, MT], F32)
        ropS = k.alloc([128, MT], F32)
        qr = k.alloc([128, 8, MT], BF16)
        vg = [k.alloc([128, 1024], F32) for _ in range(2)]
        vn = k.alloc([128, STM, 1024], BF16)
        mixT = k.alloc([128, KC, MT], BF16)
        zb = [k.alloc([128, 384], BF16) for _ in range(2)]
        t1 = [k.alloc([128, 512], F32) for _ in range(2)]
        t2 = [k.alloc([128, 384], F32) for _ in range(2)]
        ET = [k.alloc([128, 2, 512], BF16) for _ in range(2)]
        dn = [k.alloc([128, 512], F32) for _ in range(2)]
        rden = [k.alloc([128, 512], F32) for _ in range(2)]
        stats = k.alloc([128, 12], F32)
        mv = k.alloc([128, 2], F32)
        rstd = k.alloc([128, 1], F32)
        gt = t1
        ns = self.norm_scratch()
        ts = self.tail_scratch(Y=R0, ntok=MT)
        b_hT, b_uT, b_rope, b_qr, b_vn, b_mix = Buf(), Buf(), self.gb[2], Buf(), Buf(), Buf()
        ts["b_Y"] = [b_uT, b_uT]
        b_vg = [Buf(), Buf()]
        b_zb, b_t1, b_t2 = [Buf(), Buf()], [Buf(), Buf()], [Buf(), Buf()]
        b_ET, b_dn, b_rden = [Buf(), Buf()], [Buf(), Buf()], [Buf(), Buf()]
        b_st, b_gt = Buf(), b_t1
        cnt = {"pb": 0, "r": 0, "a": 0, "g": 0, "v": 0}
        cb = self.gb[0]
        pr = [(ropeP, dr["ropeP"]), (ident, dr["ident"])]
        for i in range(4):
            pr += [(maskC[:, i, :], dr["maskC"]), (maskP[:, i, :], dr["maskP"]), (maskP2[:, i, :], dr["maskP2"])]
        k.dma(k.POOL, pr, self.gb[3], writes=[cb])
        k.dma(k.SP, [(wsf, dr["sgu_wT"]), (trilf, dr["tril"]), (bsb, dr["bs_b"]), (lng, dr["lng_b"]),
                     (lnb, dr["lnb_b"]), (esink, dr["sinks_b"])], self.gb[1], writes=[cb])
        for g in range(8):
            k.op(k.DVE, lambda g=g: nc.vector.tensor_tensor(out=WmT[:, g, :], in0=wsf[:, g, :], in1=trilf, op=ALU.mult),
                 reads=[cb], writes=[cb])
        k.op(k.ACT, lambda: nc.scalar.activation(out=esink, in_=esink, func=AF.Exp), reads=[cb], writes=[cb])
        k.op(k.DVE, lambda: nc.vector.memset(vdup, 0.0), writes=[b_vdup])
        k.op(k.DVE, lambda: nc.vector.memset(kr, 0.0), writes=[b_kr])


        blocks = list(range(out_blocks[0] - 1, out_blocks[-1] + 1))
        supers = chunks_of(blocks, STM)

        def pre(sb_):
            return [lambda bi=bi, b=b: self.prenorm_to(ns, Xin, b, l, 2,
                                                       lambda c, bi=bi: hT[:, c, bi * 128:(bi + 1) * 128], b_hT)
                    for bi, b in enumerate(sb_)]
        for f_ in pre(supers[0]):
            f_()
        deferred = []
        for si, sb in enumerate(supers):
            nsb = len(sb)
            tok0 = sb[0] * 128
            ntok = nsb * 128
            subs = self.subtiles(nsb)
            k.op(k.DVE, lambda: nc.vector.memset(mixT, 0.0), writes=[b_mix])
            k.dma(k.SP, [(ropC[:, 0:ntok], dr["ropeC"][:, tok0:tok0 + ntok]),
                         (ropS[:, 0:ntok], dr["ropeS"][:, tok0:tok0 + ntok])], b_rope, writes=[b_rope])
            clist = [("q", c) for c in range(8)] + [("k", h) for h in range(2)] + [("u", c) for c in range(8)]
            cgroups = chunks_of(clist, 4)

            def pairs_fm(sl, grp):
                v = sl.rearrange("p (kc ci col) -> p kc ci col", kc=KC, ci=4)
                pr = []
                for ci, (kind, idx) in enumerate(grp):
                    if kind == "q":
                        pr.append((v[:, :, ci, :], win[:, :, idx * 128:(idx + 1) * 128]))
                    elif kind == "u":
                        pr.append((v[:, :, ci, :], win[:, :, 1280 + idx * 128:1280 + (idx + 1) * 128]))
                    else:
                        for r in range(2):
                            pr.append((v[:, :, ci, r * 64:(r + 1) * 64], win[:, :, 1024 + idx * 64:1024 + (idx + 1) * 64]))
                return pr

            items = []
            for gi, grp in enumerate(cgroups):
                for ci, (kind, idx) in enumerate(grp):
                    for n, (off, w) in enumerate(subs):
                        items.append(dict(gi=gi, ci=ci, kind=kind, idx=idx, off=off, w=w, first=(ci == 0 and n == 0)))
            slots = {0: self.wload(lambda sl: pairs_fm(sl, cgroups[0]))}

            def fm_s0(it):
                if it["kind"] == "u":
                    while deferred:
                        deferred.pop(0)()
                elif deferred:
                    deferred.pop(0)()
                gi = it["gi"]
                if it["first"] and gi + 1 < len(cgroups):
                    slots[gi + 1] = self.wload(lambda sl: pairs_fm(sl, cgroups[gi + 1]))
                cur = slots[gi]
                v = self.wslot[cur].rearrange("p (kc ci col) -> p kc ci col", kc=KC, ci=4)
                pb = cnt["pb"] % 4
                cnt["pb"] += 1
                it["pb"] = pb
                off, w = it["off"], it["w"]
                Z = bk[pb][:, 0:w]
                k.mm([(Z, v[:, c, it["ci"], :], hT[:, c, off:off + w], c == 0, c == KC - 1) for c in range(KC)],
                     reads=[self.wbuf[cur], b_hT], writes=[bb[pb]])
                if it["kind"] == "u":
                    k.op(k.ACT, lambda: nc.scalar.activation(out=uT[:, it["idx"], off:off + w], in_=Z,
                                                             func=AF.Gelu_apprx_tanh),
                         reads=[bb[pb]], writes=[b_uT])

            def fm_s1(it):
                if it["kind"] == "u":
                    return
                r = cnt["r"] % 2
                cnt["r"] += 1
                it["r"] = r
                pb, off, w = it["pb"], it["off"], it["w"]
                Z = bk[pb][:, 0:w]
                k.op(k.DVE, lambda: nc.vector.tensor_copy(out=zb[r][:, 0:w], in_=Z),
                     reads=[bb[pb]], writes=[b_zb[r]])
                k.op(k.DVE, lambda: nc.vector.tensor_tensor(out=t1[r][:, 0:w], in0=Z, in1=ropC[:, off:off + w],
                                                           op=ALU.mult),
                     reads=[bb[pb], b_rope], writes=[b_t1[r]])
                k.mm([(bk[4 + r][:, 0:w], ropeP, zb[r][:, 0:w], True, True)], reads=[b_zb[r], cb], writes=[bb[4 + r]])

            def fm_s2(it):
                if it["kind"] == "u":
                    return
                r, off, w, idx = it["r"], it["off"], it["w"], it["idx"]
                k.op(k.DVE, lambda: nc.vector.tensor_tensor(out=t2[r][:, 0:w], in0=bk[4 + r][:, 0:w],
                                                           in1=ropS[:, off:off + w], op=ALU.mult),
                     reads=[bb[4 + r], b_rope], writes=[b_t2[r]])
                if it["kind"] == "q":
                    dst, dbuf = qr[:, idx, off:off + w], b_qr
                else:
                    dst, dbuf = kr[:, idx, tok0 + off:tok0 + off + w], b_kr
                k.op(k.DVE, lambda: nc.vector.tensor_tensor(out=dst, in0=t1[r][:, 0:w], in1=t2[r][:, 0:w], op=ALU.add),
                     reads=[b_t1[r], b_t2[r]], writes=[dbuf])
            pipeline(items, [fm_s0, fm_s1, fm_s2])
            def pairs_v(sl):
                v = sl[:, 0:KC * 256].rearrange("p (kc col) -> p kc col", kc=KC)
                pr = []
                for h in range(2):
                    for r in range(2):
                        pr.append((v[:, :, h * 128 + r * 64:h * 128 + (r + 1) * 64],
                                   win[:, :, 1152 + h * 64:1152 + (h + 1) * 64]))
                return pr
            cur = self.wload(pairs_v)
            v = self.wslot[cur][:, 0:KC * 256].rearrange("p (kc col) -> p kc col", kc=KC)
            for bi, b in enumerate(sb):
                pb = cnt["pb"] % 4
                cnt["pb"] += 1
                Z = bk[pb][:, 0:256]
                k.mm([(Z, hT[:, c, bi * 128:(bi + 1) * 128], v[:, c, :], c == 0, c == KC - 1) for c in range(KC)],
                     reads=[self.wbuf[cur], b_hT], writes=[bb[pb]])
                k.op(k.ACT, lambda: nc.scalar.activation(out=vdup[:, b, :], in_=Z, func=AF.Copy),
                     reads=[bb[pb]], writes=[b_vdup])
            def pairs_sv(sl, hh):
                v = sl.rearrange("p (kc col) -> p kc col", kc=KC)
                return [(v, win[:, :, 2304 + hh * 512:2304 + (hh + 1) * 512])]
            sv = [self.wload(lambda sl, hh=hh: pairs_sv(sl, hh)) for hh in range(2)]

            def sg_s0(it):
                bi = it["bi"]
                vi = cnt["v"] % 2
                cnt["v"] += 1
                it["vi"] = vi
                for hh in range(2):
                    v = self.wslot[sv[hh]].rearrange("p (kc col) -> p kc col", kc=KC)
                    pb = cnt["pb"] % 4
                    cnt["pb"] += 1
                    Z = bk[pb][:, 0:512]
                    k.mm([(Z, hT[:, c, bi * 128:(bi + 1) * 128], v[:, c, :], c == 0, c == KC - 1) for c in range(KC)],
                         reads=[self.wbuf[sv[hh]], b_hT], writes=[bb[pb]])
                    k.op(k.ACT, lambda: nc.scalar.activation(out=vg[vi][:, hh * 512:(hh + 1) * 512], in_=Z,
                                                             func=AF.Gelu_apprx_tanh),
                         reads=[bb[pb]], writes=[b_vg[vi]])

            def sg_s1(it):
                bi, vi = it["bi"], it["vi"]
                for hh in range(2):
                    k.op(k.DVE, lambda hh=hh: nc.vector.bn_stats(out=stats[:, hh * 6:(hh + 1) * 6],
                                                                in_=vg[vi][:, hh * 512:(hh + 1) * 512]),
                         reads=[b_vg[vi]], writes=[b_st])
                k.op(k.DVE, lambda: nc.vector.bn_aggr(out=mv, in_=stats), reads=[b_st], writes=[b_st])
                k.op(k.ACT, lambda: nc.scalar.activation(out=rstd, in_=mv[:, 1:2], func=AF.Sqrt, bias=self.eps_t, scale=1.0),
                     reads=[b_st, self.cbuf], writes=[b_st])
                k.op(k.DVE, lambda: nc.vector.reciprocal(out=rstd, in_=rstd), reads=[b_st], writes=[b_st])
                k.op(k.DVE, lambda: nc.vector.tensor_scalar(out=vg[vi], in0=vg[vi], scalar1=mv[:, 0:1], scalar2=rstd,
                                                           op0=ALU.subtract, op1=ALU.mult),
                     reads=[b_vg[vi], b_st], writes=[b_vg[vi]])
                k.op(k.DVE, lambda: nc.vector.tensor_tensor(out=vg[vi], in0=vg[vi], in1=lng, op=ALU.mult),
                     reads=[b_vg[vi], cb], writes=[b_vg[vi]])
                k.op(k.DVE, lambda: nc.vector.tensor_tensor(out=vn[:, bi, :], in0=vg[vi], in1=lnb, op=ALU.add),
                     reads=[b_vg[vi], cb], writes=[b_vn])

            def sg_s2(it):
                bi = it["bi"]
                for hh in range(2):
                    pb = 4 + cnt["g"] % 2
                    for g4 in range(4):
                        g = hh * 4 + g4
                        k.mm([(bk[pb][:, g4 * 128:(g4 + 1) * 128], vn[:, bi, g * 128:(g + 1) * 128], WmT[:, g, :], True, True)],
                             reads=[b_vn, cb], writes=[bb[pb]])
                    gi_ = cnt["g"] % 2
                    cnt["g"] += 1
                    gv = gt[gi_].rearrange("p (a b) -> p a b", a=4)
                    k.op(k.DVE, lambda: nc.vector.tensor_tensor(out=gv, in0=bk[pb].rearrange("p (a b) -> p a b", a=4),
                                                               in1=bsb[:, hh * 4:(hh + 1) * 4, :], op=ALU.add),
                         reads=[bb[pb], cb], writes=[b_gt[gi_]])
                    k.op(k.DVE, lambda: nc.vector.tensor_tensor(
                        out=mixT[:, 8 + hh * 4:8 + (hh + 1) * 4, bi * 128:(bi + 1) * 128], in0=gv,
                        in1=uT[:, hh * 4:(hh + 1) * 4, bi * 128:(bi + 1) * 128], op=ALU.mult),
                        reads=[b_gt[gi_], b_uT], writes=[b_mix])
            pipeline([dict(bi=bi, b=b) for bi, b in enumerate(sb)], [sg_s0, sg_s1, sg_s2])
            aitems = []
            for bi, b in enumerate(sb):
                if b < out_blocks[0]:
                    continue
                for g in range(2):
                    for par in range(2):
                        aitems.append(dict(bi=bi, b=b, g=g, par=par))

            def at_s0(it):
                bi, b, g, par = it["bi"], it["b"], it["g"], it["par"]
                a = cnt["a"] % 2
                cnt["a"] += 1
                it["a"] = a
                base = a * 4
                ps = slice(par * 64, (par + 1) * 64)
                mP = maskP2 if b == 2 else maskP
                rq = qr[ps, g * 4:(g + 1) * 4, bi * 128:(bi + 1) * 128]
                for kb, (kblk, msk) in enumerate(((b - 1, mP), (b, maskC))):
                    k.mm([(bk[base + kb], kr[ps, g, kblk * 128:(kblk + 1) * 128], rq, True, False),
                          (bk[base + kb], ident, msk.rearrange("p a b -> p (a b)"), False, True)],
                         reads=[b_kr, b_qr, cb], writes=[bb[base + kb]])
                    k.op(k.ACT, lambda kb=kb: nc.scalar.activation(out=ET[a][:, kb, :], in_=bk[base + kb],
                                                                 func=AF.Exp, scale=0.125),
                         reads=[bb[base + kb]], writes=[b_ET[a]])

            def at_s1(it):
                b, g, par, a = it["b"], it["g"], it["par"], it["a"]
                base = a * 4
                k.mm([(bk[base + 2], self.ones, ET[a][:, 0, :], True, False),
                      (bk[base + 2], self.ones, ET[a][:, 1, :], False, True)],
                     reads=[b_ET[a], self.cbuf], writes=[bb[base + 2]])
                k.mm([(bk[base + 3], vdup[:, b - 1, g * 128:(g + 1) * 128], ET[a][:, 0, :], True, False),
                      (bk[base + 3], vdup[:, b, g * 128:(g + 1) * 128], ET[a][:, 1, :], False, True)],
                     reads=[b_ET[a], b_vdup], writes=[bb[base + 3]])
                hs = (g * 2 + par) * 4
                k.op(k.DVE, lambda: nc.vector.tensor_tensor(
                    out=dn[a].rearrange("p (a b) -> p a b", a=4), in0=bk[base + 2].rearrange("p (a b) -> p a b", a=4),
                    in1=esink[:, hs:hs + 4].unsqueeze(2).to_broadcast([128, 4, 128]), op=ALU.add),
                    reads=[bb[base + 2], cb], writes=[b_dn[a]])

            def at_s2(it):
                bi, g, par, a = it["bi"], it["g"], it["par"], it["a"]
                base = a * 4
                ps = slice(par * 64, (par + 1) * 64)
                k.op(k.ACT, lambda: nc.scalar.activation(out=dn[a], in_=dn[a], func=AF.Ln),
                     reads=[b_dn[a]], writes=[b_dn[a]])
                k.op(k.ACT, lambda: nc.scalar.activation(out=rden[a], in_=dn[a], func=AF.Exp, scale=-1.0),
                     reads=[b_dn[a]], writes=[b_rden[a]])
                k.op(k.DVE, lambda: nc.vector.tensor_tensor(
                    out=mixT[ps, g * 4:(g + 1) * 4, bi * 128:(bi + 1) * 128],
                    in0=bk[base + 3][ps, :].rearrange("p (a b) -> p a b", a=4),
                    in1=rden[a][ps, :].rearrange("p (a b) -> p a b", a=4), op=ALU.mult),
                    reads=[bb[base + 3], b_rden[a]], writes=[b_mix])
            pipeline(aitems, [at_s0, at_s1, at_s2])
            osb = [b for b in sb if b >= out_blocks[0]]
            skip = len(sb) - len(osb)
            assert skip in (0, 1)
            if skip:
                inT = mixT[:, :, 128:]
            else:
                inT = mixT

            def wp(sl, ocs):
                v = sl.rearrange("p (kc col) -> p kc col", kc=KC)
                return [(v[:, :, 0:len(ocs) * 128], wout[:, :, ocs[0] * 128:(ocs[-1] + 1) * 128])]

            def ml(sl, oc, oi):
                v = sl.rearrange("p (kc col) -> p kc col", kc=KC)
                return [(v[:, c, oi * 128:(oi + 1) * 128], c) for c in range(KC)]
            while deferred:
                deferred.pop(0)()
            deferred = self.proj_tail(osb, inT, [b_mix, b_mix], chunks_of(list(range(KC)), 4), wp, ml, l, 3, 1.0, Xin, Xout,
                                      ts, interleave=(pre(supers[si + 1]) if si + 1 < len(supers) else None),
                                      defer_tail=True)
        while deferred:
            deferred.pop(0)()
        k.barrier()
        k.top = mark

    def cross(self, l, Xin, Xout, blocks, xout_index=None):
        k = self.k
        nc = k.nc
        bk, bb = k.banks, k.bankbuf
        dr = k.dram
        mark = k.top
        wq = dr["x_wq"][l].rearrange("(c p) f -> p c f", p=128)
        wk = dr["x_wk"][l].rearrange("(c p) f -> p c f", p=128)
        wv = dr["x_wv"][l].rearrange("(c p) f -> p c f", p=128)
        wo = dr["x_wo"][l].rearrange("(c p) f -> p c f", p=128)
        memf = k.alloc([128, KC, 256], F32)
        memn = k.alloc([128, KC, 256], BF16)
        msq = k.alloc([128, KC, 256], BF16)
        msr = k.alloc([128, 256], F32)
        mg = k.alloc([128, KC], F32)
        KT = k.alloc([128, 4, 256], BF16)
        Vm = k.alloc([128, 2, 512], BF16)
        cb = self.gb[0]
        b_m = Buf()
        k.dma(k.SP, [(memf, dr["memT"]), (mg, dr["memnorm_t"][:, l * KC:(l + 1) * KC])], cb, writes=[cb])
        k.op(k.ACT, lambda: nc.scalar.activation(out=msq, in_=memf, func=AF.Square), reads=[cb], writes=[b_m])
        k.mm([(bk[7][:, 0:256], self.ones, msq[:, c, :], c == 0, c == KC - 1) for c in range(KC)],
             reads=[b_m, self.cbuf], writes=[bb[7]])
        k.op(k.ACT, lambda: nc.scalar.activation(out=msr, in_=bk[7][:, 0:256], func=AF.Sqrt, bias=self.eps_t, scale=1.0 / D),
             reads=[bb[7], self.cbuf], writes=[b_m])
        k.op(k.DVE, lambda: nc.vector.reciprocal(out=msr, in_=msr), reads=[b_m], writes=[b_m])
        for c in range(KC):
            k.op(k.DVE, lambda c=c: nc.vector.scalar_tensor_tensor(out=memn[:, c, :], in0=memf[:, c, :], scalar=mg[:, c:c + 1],
                                                                  in1=msr, op0=ALU.mult, op1=ALU.mult),
                 reads=[cb, b_m], writes=[b_m])
        full = lambda w_: (lambda sl: [(sl.rearrange("p (kc col) -> p kc col", kc=KC), w_)])
        cur = self.wload(full(wk))
        nxt = self.wload(full(wv))
        v = self.wslot[cur].rearrange("p (kc col) -> p kc col", kc=KC)
        for h in range(4):
            pb = h % 4
            k.mm([(bk[pb][:, 0:256], v[:, c, h * 128:(h + 1) * 128], memn[:, c, :], c == 0, c == KC - 1) for c in range(KC)],
                 reads=[self.wbuf[cur], b_m], writes=[bb[pb]])
            k.op(k.ACT, lambda: nc.scalar.activation(out=KT[:, h, :], in_=bk[pb][:, 0:256], func=AF.Copy),
                 reads=[bb[pb]], writes=[b_m])
        cur = nxt
        v = self.wslot[cur].rearrange("p (kc col) -> p kc col", kc=KC)
        for m in range(2):
            pb = m
            k.mm([(bk[pb], memn[:, c, m * 128:(m + 1) * 128], v[:, c, :], c == 0, c == KC - 1) for c in range(KC)],
                 reads=[self.wbuf[cur], b_m], writes=[bb[pb]])
            k.op(k.ACT, lambda: nc.scalar.activation(out=Vm[:, m, :], in_=bk[pb], func=AF.Copy),
                 reads=[bb[pb]], writes=[b_m])
        hT = k.alloc([128, KC, 768], BF16)
        qT = k.alloc([128, 4, 768], BF16)
        oT = k.alloc([128, 4, 768], BF16)
        ET = [k.alloc([128, 2, 384], BF16) for _ in range(2)]
        dn = [k.alloc([128, 384], F32) for _ in range(2)]
        rden = [k.alloc([128, 384], F32) for _ in range(2)]
        ns = self.norm_scratch()
        ts = self.tail_scratch()
        b_hT, b_qT, b_oT = Buf(), Buf(), [Buf(), Buf()]
        b_ET, b_dn, b_rden = [Buf(), Buf()], [Buf(), Buf()], [Buf(), Buf()]
        cnt = {"a": 0}
        sc = 128.0 ** -0.5
        supers = chunks_of(blocks, 6)
        for bi, b in enumerate(supers[0]):
            self.prenorm_to(ns, Xin, b, l, 4, lambda c, bi=bi: hT[:, c, bi * 128:(bi + 1) * 128], b_hT)
        deferred = []
        for si, sb in enumerate(supers):
            subs = self.subtiles(len(sb))
            cur = self.wload(full(wq))
            v = self.wslot[cur].rearrange("p (kc col) -> p kc col", kc=KC)
            for h in range(4):
                for n, (off, w) in enumerate(subs):
                    pb = (h * 2 + n) % 4
                    k.mm([(bk[pb][:, 0:w], v[:, c, h * 128:(h + 1) * 128], hT[:, c, off:off + w], c == 0, c == KC - 1)
                          for c in range(KC)], reads=[self.wbuf[cur], b_hT], writes=[bb[pb]])
                    k.op(k.ACT, lambda: nc.scalar.activation(out=qT[:, h, off:off + w], in_=bk[pb][:, 0:w], func=AF.Copy),
                         reads=[bb[pb]], writes=[b_qT])
            citems = [dict(h=h, n=n, off=off, w=w) for h in range(4) for n, (off, w) in enumerate(subs)]

            def ca_s0(it):
                for _ in range(2):
                    if deferred:
                        deferred.pop(0)()
                h, off, w = it["h"], it["off"], it["w"]
                a = cnt["a"] % 2
                cnt["a"] += 1
                it["a"] = a
                base = a * 4
                for m in range(2):
                    k.mm([(bk[base + m][:, 0:w], KT[:, h, m * 128:(m + 1) * 128], qT[:, h, off:off + w], True, True)],
                         reads=[b_m, b_qT], writes=[bb[base + m]])
                    k.op(k.ACT, lambda m=m: nc.scalar.activation(out=ET[a][:, m, 0:w], in_=bk[base + m][:, 0:w],
                                                               func=AF.Exp, scale=sc),
                         reads=[bb[base + m]], writes=[b_ET[a]])

            def ca_s1(it):
                h, w, a = it["h"], it["w"], it["a"]
                base = a * 4
                k.mm([(bk[base + 2][:, 0:w], self.ones, ET[a][:, 0, 0:w], True, False),
                      (bk[base + 2][:, 0:w], self.ones, ET[a][:, 1, 0:w], False, True)],
                     reads=[b_ET[a], self.cbuf], writes=[bb[base + 2]])
                k.mm([(bk[base + 3][:, 0:w], Vm[:, 0, h * 128:(h + 1) * 128], ET[a][:, 0, 0:w], True, False),
                      (bk[base + 3][:, 0:w], Vm[:, 1, h * 128:(h + 1) * 128], ET[a][:, 1, 0:w], False, True)],
                     reads=[b_ET[a], b_m], writes=[bb[base + 3]])
                k.op(k.ACT, lambda: nc.scalar.activation(out=dn[a][:, 0:w], in_=bk[base + 2][:, 0:w], func=AF.Ln),
                     reads=[bb[base + 2]], writes=[b_dn[a]])

            def ca_s2(it):
                h, n, off, w, a = it["h"], it["n"], it["off"], it["w"], it["a"]
                base = a * 4
                k.op(k.ACT, lambda: nc.scalar.activation(out=rden[a][:, 0:w], in_=dn[a][:, 0:w], func=AF.Exp, scale=-1.0),
                     reads=[b_dn[a]], writes=[b_rden[a]])
                k.op(k.DVE, lambda: nc.vector.tensor_tensor(out=oT[:, h, off:off + w], in0=bk[base + 3][:, 0:w],
                                                           in1=rden[a][:, 0:w], op=ALU.mult),
                     reads=[bb[base + 3], b_rden[a]], writes=[b_oT[n]])
            pipeline(citems, [ca_s0, ca_s1, ca_s2])
            while deferred:
                deferred.pop(0)()

            def wp(sl, ocs):
                vv = sl.rearrange("p (kc col) -> p kc col", kc=4)
                return [(vv, wo)]

            def ml(sl, oc, oi):
                vv = sl.rearrange("p (kc col) -> p kc col", kc=4)
                return [(vv[:, c, oc * 128:(oc + 1) * 128], c) for c in range(4)]
            inter = []
            if si + 1 < len(supers):
                for bi, b in enumerate(supers[si + 1]):
                    inter.append(lambda bi=bi, b=b: self.prenorm_to(
                        ns, Xin, b, l, 4, lambda c, bi=bi: hT[:, c, bi * 128:(bi + 1) * 128], b_hT))
            deferred = self.proj_tail(sb, oT, b_oT, [list(range(KC))], wp, ml, l, 5, 1.0, Xin, Xout, ts,
                                      xout_index=xout_index, interleave=inter, defer_tail=True)
        while deferred:
            deferred.pop(0)()
        k.barrier()
        k.top = mark

    def pool1(self, Xin, Xout, out_blocks):
        k = self.k
        nc = k.nc
        dr = k.dram
        l = 1
        mark = k.top
        pw = dr["pool_w"][0].rearrange("g (cc p) d -> p g cc d", p=128)
        psc = k.alloc([128, KC], F32)
        flag = k.alloc([128, 1], F32)
        tab = k.alloc([128, 4, 16], F32)
        cb = self.gb[0]
        k.dma(k.SP, [(psc, dr["poolscale_t"]), (flag, dr["poolflag"]), (tab, dr["pooltab"])], cb, writes=[cb])
        hf = k.alloc([128, KC, 896], F32)
        ta = [k.alloc([128, 896], F32) for _ in range(2)]
        pooled = k.alloc([128, KC, 768], BF16)
        fx = k.alloc([128, 16], F32)
        ns = self.norm_scratch()
        ts = self.tail_scratch()
        b_hf, b_ta, b_pool, b_fx = Buf(), [Buf(), Buf()], [Buf(), Buf()], Buf()
        for i in range(2):
            k.op(k.DVE, lambda i=i: nc.vector.memset(ta[i], 0.0), writes=[b_ta[i]])
        psupers = chunks_of(out_blocks, 6)

        def pre(sb_):
            first_ = (sb_[0] == 2)
            return [lambda bi=bi, b=b: self.prenorm_to(
                ns, Xin, b, l, 2, lambda c, bi=bi: hf[:, c, bi * 128:(bi + 1) * 128], b_hf,
                flag=(flag if (first_ and bi == 0) else None), flag_buf=cb)
                for bi, b in enumerate([sb_[0] - 1] + sb_)]
        for f_ in pre(psupers[0]):
            f_()
        deferred = []
        for si, sb in enumerate(psupers):
            nsb = len(sb)
            W = (nsb + 1) * 128
            first = (sb[0] == 2)
            for c in range(KC):
                if deferred:
                    deferred.pop(0)()
                gi = c // 4
                src = hf[:, c, :]
                sbuf = b_hf
                sh = 1
                for lev in range(gi + 1):
                    d = ta[lev % 2]
                    k.op(k.DVE, lambda: nc.vector.tensor_tensor(out=d[:, sh:W], in0=src[:, sh:W], in1=src[:, 0:W - sh],
                                                               op=ALU.add),
                         reads=[sbuf], writes=[b_ta[lev % 2]])
                    src, sbuf = d, b_ta[lev % 2]
                    sh *= 2
                wnd = 2 ** (gi + 1)
                k.op(k.DVE, lambda: nc.vector.scalar_tensor_tensor(out=pooled[:, c, 0:nsb * 128], in0=src[:, 128:W],
                                                                  scalar=1.0 / wnd, in1=hf[:, c, 128:W],
                                                                  op0=ALU.mult, op1=ALU.subtract),
                     reads=[sbuf, b_hf], writes=[b_pool[0]])
                if first:
                    k.op(k.DVE, lambda: nc.vector.tensor_tensor(out=fx, in0=src[:, 128:144], in1=tab[:, gi, :], op=ALU.mult),
                         reads=[sbuf, cb], writes=[b_fx])
                    k.op(k.DVE, lambda: nc.vector.tensor_tensor(out=pooled[:, c, 0:16], in0=fx, in1=hf[:, c, 128:144],
                                                               op=ALU.subtract),
                         reads=[b_fx, b_hf], writes=[b_pool[0]])

            def wp(sl, ocs):
                vv = sl.rearrange("p (g cc col) -> p g cc col", g=4, cc=4)
                return [(vv, pw)]

            def ml(sl, oc, oi):
                vv = sl.rearrange("p (g cc col) -> p g cc col", g=4, cc=4)
                g = oc // 4
                return [(vv[:, g, cc, (oc % 4) * 128:(oc % 4 + 1) * 128], g * 4 + cc) for cc in range(4)]
            while deferred:
                deferred.pop(0)()
            deferred = self.proj_tail(sb, pooled, [b_pool[0], b_pool[0]], [list(range(KC))], wp, ml, l, 3, 1.0, Xin, Xout,
                                      ts, scale_of=lambda oc: psc[:, oc:oc + 1], xreads=[cb],
                                      interleave=(pre(psupers[si + 1]) if si + 1 < len(psupers) else None),
                                      defer_tail=True)
        while deferred:
            deferred.pop(0)()
        k.barrier()
        k.top = mark


def x_layout(xc):
    nb = xc.shape[0] // 128
    return np.ascontiguousarray(xc.reshape(nb, 128, KC, 128).transpose(0, 3, 2, 1))


def x_unlayout(xl):
    nb = xl.shape[0]
    return np.ascontiguousarray(xl.transpose(0, 3, 2, 1).reshape(nb * 128, D))


def build_full(nblk=NBLK, dff=DFF):
    p = Prog(nblk=nblk, dff=dff)
    tok = nblk * 128
    X0 = p.din("x0", [nblk, 128, KC, 128])
    p.din("memT", [128, KC, MEM])
    p.din("norms_t", [128, 256])
    p.din("memnorm_t", [128, 32])
    p.din("poolscale_t", [128, 16])
    for w in (1, 2):
        p.din(f"ffn{w}_wg", [2, D, dff]); p.din(f"ffn{w}_wu", [2, D, dff]); p.din(f"ffn{w}_wd", [2, dff, D])
    p.din("x_wq", [2, D, 512]); p.din("x_wk", [2, D, 512]); p.din("x_wv", [2, D, 512]); p.din("x_wo", [2, 512, D])
    p.din("mix_w_in", [1, D, 3328]); p.din("mix_w_out", [1, D, D]); p.din("pool_w", [1, 4, 512, 512])
    p.din("sinks_b", [128, 16]); p.din("lng_b", [128, 1024]); p.din("lnb_b", [128, 1024])
    p.din("sgu_wT", [128, 8, 128]); p.din("bs_b", [128, 8, 128]); p.din("tril", [128, 128])
    p.din("ropeC", [128, tok]); p.din("ropeS", [128, tok]); p.din("ropeP", [128, 128]); p.din("ident", [128, 128])
    p.din("maskC", [128, 128]); p.din("maskP", [128, 128]); p.din("maskP2", [128, 128])
    p.din("poolflag", [128, 1]); p.din("pooltab", [128, 4, 16])
    OUT = p.dout("out", [nblk - 2, 128, KC, 128])
    XA = p.dscratch("xa", [nblk, 128, KC, 128])
    XB = p.dscratch("xb", [nblk, 128, KC, 128])
    allb = list(range(nblk))
    b1 = list(range(1, nblk))
    b2 = list(range(2, nblk))
    p.load_consts()
    p.ffn(0, 1, X0, XA, allb)
    p.mixer0(XA, XB, b1)
    p.cross(0, XB, XA, b1)
    p.ffn(0, 2, XA, XB, b1)
    p.ffn(1, 1, XB, XA, b1)
    p.pool1(XA, XB, b2)
    p.cross(1, XB, XA, b2)
    p.ffn(1, 2, XA, OUT, b2, xout_index=lambda b: b - 2)
    p.k.finish()
    return p


def host_consts(nblk, pos0, seq_start):
    tok = nblk * 128
    pos = (np.arange(tok) + pos0).astype(np.float32)
    half = 8
    inv = (np.float32(500000.0) ** (-np.arange(half, dtype=np.float32) * np.float32(2.0) / np.float32(16))).astype(np.float32)
    ang = (pos[:, None] * inv[None, :]).astype(np.float32)
    cos = np.cos(ang).astype(np.float32).T
    sin = np.sin(ang).astype(np.float32).T
    C = np.ones((128, tok), np.float32)
    S = np.zeros((128, tok), np.float32)
    PT = np.zeros((128, 128), np.float32)
    for p_ in range(128):
        d = p_ % 64
        if d < 8:
            C[p_] = cos[d]; S[p_] = -sin[d]; PT[p_ + 8, p_] = 1.0
        elif d < 16:
            C[p_] = cos[d - 8]; S[p_] = sin[d - 8]; PT[p_ - 8, p_] = 1.0
    j = np.arange(128)[:, None]
    i = np.arange(128)[None, :]
    NEG = np.float32(-30000.0)
    maskC = np.where(j <= i, np.float32(0), NEG).astype(np.float32)
    maskP = np.where(j > i, np.float32(0), NEG).astype(np.float32)
    maskP2 = np.full((128, 128), NEG, np.float32) if seq_start else maskP.copy()
    tril = (j <= i).astype(np.float32)
    flag = np.full((128, 1), 0.0 if seq_start else 1.0, np.float32)
    tab = np.zeros((128, 4, 16), np.float32)
    for gi, w in enumerate((2, 4, 8, 16)):
        for t in range(16):
            cnt = min(t + 1, w) if seq_start else w
            tab[:, gi, t] = np.float32(1.0) / np.float32(cnt)
    return dict(ropeC=C, ropeS=S, ropeP=PT, ident=np.eye(128, dtype=np.float32), maskC=maskC, maskP=maskP,
                maskP2=maskP2, tril=tril, poolflag=flag, pooltab=tab)


def host_shared(inp):
    f = lambda a: np.ascontiguousarray(np.asarray(a, dtype=np.float32))
    sh = {}
    norms = f(inp["norms"])
    sh["norms_t"] = f(norms.reshape(2, 8, KC, 128).transpose(3, 0, 1, 2).reshape(128, 256))
    sh["memnorm_t"] = f(f(inp["mem_norm"]).reshape(2, KC, 128).transpose(2, 0, 1).reshape(128, 32))
    sh["poolscale_t"] = f(f(inp["pool_scale"])[0].reshape(KC, 128).T)
    for nm in ("ffn1_wg", "ffn1_wu", "ffn1_wd", "ffn2_wg", "ffn2_wu", "ffn2_wd", "x_wq", "x_wk", "x_wv", "x_wo",
               "mix_w_in", "mix_w_out", "pool_w"):
        sh[nm] = f(inp[nm])
    sinks = f(inp["attn_sinks"])[0]
    perm = np.zeros(16, np.float32)
    for g in range(2):
        for par in range(2):
            for i in range(4):
                perm[(g * 2 + par) * 4 + i] = sinks[2 * (g * 4 + i) + par]
    sh["sinks_b"] = f(np.broadcast_to(perm[None, :], (128, 16)))
    sh["lng_b"] = f(np.broadcast_to(f(inp["sgu_ln_g"])[0][None, :], (128, 1024)))
    sh["lnb_b"] = f(np.broadcast_to(f(inp["sgu_ln_b"])[0][None, :], (128, 1024)))
    sh["sgu_wT"] = f(f(inp["sgu_w"])[0].transpose(2, 0, 1))
    sh["bs_b"] = f(np.broadcast_to(f(inp["sgu_b"])[0][None, :, :], (128, 8, 128)))
    return sh


def core_inputs(inp, sh, c, nblk=NBLK, seq=SEQ):
    own = (nblk - 2) * 128
    per_seq = seq // own
    b, hf = c // per_seq, c % per_seq
    x = np.asarray(inp["x"], dtype=np.float32)
    start = hf * own - 256
    xc = np.zeros((nblk * 128, D), np.float32)
    lo = max(start, 0)
    xc[lo - start:] = x[b, lo:start + nblk * 128]
    m = {"x0": x_layout(xc)}
    mem = np.asarray(inp["mem"], dtype=np.float32)[b]
    m["memT"] = np.ascontiguousarray(mem.reshape(MEM, KC, 128).transpose(2, 1, 0))
    m.update(sh)
    m.update(host_consts(nblk, start, hf == 0))
    return m


def kernel(**inputs):
    p = build_full()
    sh = host_shared(inputs)
    in_maps = [core_inputs(inputs, sh, c) for c in range(NCORES)]
    res = run_bass_kernel_spmd(p.k.nc, in_maps, core_ids=list(range(NCORES)))
    out = np.zeros((BATCH, SEQ, D), np.float32)
    for c in range(NCORES):
        b, hf = c // 2, c % 2
        out[b, hf * HALF:(hf + 1) * HALF] = x_unlayout(np.asarray(res.results[c]["out"]))
    return out
```
